# Optimizing a Trainium2 kernel written in Bass

```python
import math
import jax, jax.numpy as jnp
from jax import lax
import numpy as np

D_MODEL = 1024
BATCH = 32
SEQ = 256
DEPTH = 2
DEC_BATCH = 2
DEC_SEQ = 1024
PAST_LEN = 512

GRID_W = 64
HEAD_DIM = 64
N_Q_HEADS = 8
N_KV_HEADS = 2
Q_PER_KV = N_Q_HEADS // N_KV_HEADS
ATTN_W = N_Q_HEADS * HEAD_DIM
KV_W = N_KV_HEADS * HEAD_DIM
Q_BLOCK = 128
ROPE_THETA = 10000.0
ATTN_SCALE = HEAD_DIM ** -0.5
HY_W = 512
HY_SHORT = 3
HY_BANDS = 16
HY_EMB = 2 * HY_BANDS + 1
HY_FH = 64
HY_DECAY_MIN = math.log(100.0) / 1.5
HY_DECAY_MAX = math.log(100.0) / 0.3
LRU_W = 512
LRU_BLOCKS = 8
LRU_BS = LRU_W // LRU_BLOCKS
LRU_CONV = 4
LRU_C = 8.0
D_MIX = ATTN_W + HY_W + LRU_W
PROJ_SIZES = (ATTN_W, KV_W, KV_W, ATTN_W, HY_W, HY_W, HY_W, HY_W, LRU_W, LRU_W)
D_PROJ = sum(PROJ_SIZES)
PROJ_SPLITS = tuple(int(s) for s in np.cumsum(PROJ_SIZES)[:-1])
EPS = 1e-6
F32 = jnp.float32

kernel_name = 'hymba_flow_hybrid_step'


def rmsnorm(x, g):
    xf = x.astype(F32)
    y = xf * lax.rsqrt(jnp.mean(xf * xf, axis=-1, keepdims=True) + EPS)
    return (y * g.astype(F32)).astype(x.dtype)


def dwconv_centred(x, w, b):
    K = w.shape[0]
    L = x.shape[1]
    left = (K - 1) // 2
    xp = jnp.pad(x, ((0, 0), (left, K - 1 - left), (0, 0)))
    y = b
    for j in range(K):
        y = y + xp[:, j:j + L] * w[j]
    return y


def axial_rope(x):
    L = x.shape[1]
    rows = L // GRID_W
    row = jnp.repeat(jnp.arange(rows), GRID_W).astype(F32)
    col = jnp.tile(jnp.arange(GRID_W), rows).astype(F32)
    n_freq = HEAD_DIM // 4
    inv = ROPE_THETA ** (-jnp.arange(n_freq, dtype=F32) / n_freq)
    ang = jnp.concatenate([row[:, None] * inv, col[:, None] * inv], axis=-1)
    cos = jnp.cos(ang)[None, :, None, :]
    sin = jnp.sin(ang)[None, :, None, :]
    xf = x.astype(F32).reshape(x.shape[:-1] + (HEAD_DIM // 2, 2))
    x1, x2 = xf[..., 0], xf[..., 1]
    out = jnp.stack([x1 * cos - x2 * sin, x1 * sin + x2 * cos], axis=-1)
    return out.reshape(x.shape).astype(x.dtype)


def blocked_attention(q, k, v):
    B, Lq, H, Dh = q.shape
    nb = Lq // Q_BLOCK
    qb = q.reshape(B, nb, Q_BLOCK, N_KV_HEADS, Q_PER_KV, Dh).transpose(1, 0, 2, 3, 4, 5)

    def one_block(qblk):
        s = jnp.einsum('bqkgd,bskd->bkgqs', qblk, k).astype(F32) * ATTN_SCALE
        p = jax.nn.softmax(s, axis=-1).astype(v.dtype)
        return jnp.einsum('bkgqs,bskd->bqkgd', p, v)

    o = lax.map(one_block, qb)
    return o.transpose(1, 0, 2, 3, 4, 5).reshape(B, Lq, H * Dh)


def hyena_filter(L, p):
    t = jnp.linspace(0.0, 1.0, L, dtype=F32)[:, None]
    w = 2.0 * math.pi * jnp.arange(L, dtype=F32)[:, None] / L
    f = jnp.linspace(1e-4, HY_BANDS - 1, HY_BANDS, dtype=F32)[None, :]
    feats = jnp.concatenate([t, jnp.cos(f * w), -jnp.sin(f * w)], axis=-1)
    freq = p['hy_filt_freq'].astype(F32)
    hdn = jnp.sin(freq[0] * (feats @ p['hy_filt_w1'].astype(F32) + p['hy_filt_b1'].astype(F32)))
    hdn = jnp.sin(freq[1] * (hdn @ p['hy_filt_w2'].astype(F32) + p['hy_filt_b2'].astype(F32)))
    h = (hdn @ p['hy_filt_w3'].astype(F32)) * jnp.exp(-t * p['hy_filt_decay'].astype(F32))
    h_fwd, h_bwd = h[:, :HY_W], h[:, HY_W:]
    filt = jnp.concatenate([h_fwd, jnp.zeros((1, HY_W), F32), h_bwd[:0:-1]], axis=0)
    return filt / jnp.sum(jnp.abs(filt), axis=0, keepdims=True)


def long_conv(z, filt, bias):
    L = z.shape[1]
    zf = z.astype(F32)
    y = jnp.fft.irfft(jnp.fft.rfft(zf, n=2 * L, axis=1) * jnp.fft.rfft(filt, n=2 * L, axis=0)[None],
                      n=2 * L, axis=1)[:, :L]
    return (y + zf * bias.astype(F32)).astype(z.dtype)


def _lin_combine(e1, e2):
    a1, b1 = e1
    a2, b2 = e2
    return a1 * a2, a2 * b1 + b2


def rg_lru(xc, wa, ba, wx, bx, lam, h0, reverse):
    B, L, W = xc.shape
    xf = xc.astype(F32)
    xb = xf.reshape(B, L, LRU_BLOCKS, LRU_BS)
    r = jax.nn.sigmoid(jnp.einsum('blnc,ncd->blnd', xb, wa.astype(F32)).reshape(B, L, W) + ba.astype(F32))
    i = jax.nn.sigmoid(jnp.einsum('blnc,ncd->blnd', xb, wx.astype(F32)).reshape(B, L, W) + bx.astype(F32))
    log_a = -LRU_C * r * jax.nn.softplus(-lam.astype(F32))
    a = jnp.exp(log_a)
    b = jnp.sqrt(-jnp.expm1(2.0 * log_a)) * (i * xf)
    if reverse:
        a, b = a[:, ::-1], b[:, ::-1]
    b = b.at[:, 0].add(a[:, 0] * h0.astype(F32))
    _, hs = lax.associative_scan(_lin_combine, (a, b), axis=1)
    h_last = hs[:, -1]
    if reverse:
        hs = hs[:, ::-1]
    return hs, h_last


def mixer_layer(x, cvec, p, ctx_k=None, ctx_v=None, ctx_h=None):
    B, L, _ = x.shape
    latent = ctx_k is not None
    mod = jnp.dot(jax.nn.silu(cvec.astype(F32)), p['w_ada'].astype(F32)) + p['b_ada'].astype(F32)
    shift, scale, gate = jnp.split(mod[:, None, :], 3, axis=-1)
    h = (rmsnorm(x, p['norm_g']).astype(F32) * (1.0 + scale) + shift).astype(x.dtype)
    u = jnp.einsum('bld,de->ble', h, p['w_in'])
    q, k, v, g_attn, hy_x0, hy_x1, hy_v, g_hy, lru_x, g_lru = jnp.split(u, PROJ_SPLITS, axis=-1)

    q = rmsnorm(q.reshape(B, L, N_Q_HEADS, HEAD_DIM), p['q_norm_g'])
    k = rmsnorm(k.reshape(B, L, N_KV_HEADS, HEAD_DIM), p['k_norm_g'])
    v = v.reshape(B, L, N_KV_HEADS, HEAD_DIM)
    if latent:
        q = axial_rope(q)
        k_all = jnp.concatenate([ctx_k.astype(k.dtype), axial_rope(k)], axis=1)
        v_all = jnp.concatenate([ctx_v.astype(v.dtype), v], axis=1)
    else:
        k_all, v_all = k, v
    y_attn = blocked_attention(q, k_all, v_all)

    hz = dwconv_centred(jnp.concatenate([hy_x0, hy_x1, hy_v], axis=-1), p['hy_short_w'], p['hy_short_b'])
    x0, x1, hv = jnp.split(hz, 3, axis=-1)
    y_hy = x0 * long_conv(x1 * hv, hyena_filter(L, p), p['hy_bias'])

    xc = dwconv_centred(lru_x, p['lru_conv_w'], p['lru_conv_b'])
    h0 = ctx_h if latent else jnp.zeros((B, 2, LRU_W), F32)
    y_f, h_f = rg_lru(xc, p['lru_wa'][0], p['lru_ba'][0], p['lru_wx'][0], p['lru_bx'][0],
                      p['lru_lambda'][0], h0[:, 0], False)
    y_b, h_b = rg_lru(xc, p['lru_wa'][1], p['lru_ba'][1], p['lru_wx'][1], p['lru_bx'][1],
                      p['lru_lambda'][1], h0[:, 1], True)
    y_lru = (y_f + y_b).astype(x.dtype)

    mix = jnp.concatenate([y_attn * jax.nn.silu(g_attn), y_hy * jax.nn.silu(g_hy),
                           y_lru * jax.nn.silu(g_lru)], axis=-1)
    out = jnp.einsum('ble,ed->bld', mix, p['w_out'])
    x_new = (x.astype(F32) + gate * out.astype(F32)).astype(x.dtype)
    return x_new, k, v, jnp.stack([h_f, h_b], axis=1).astype(x.dtype)


def setup_inputs(seed: int = 0) -> dict:
    key = jax.random.key(seed)
    ks = iter(jax.random.split(key, 48))

    def nrm(shape, s):
        return jax.random.normal(next(ks), shape, F32) * s

    a_c = jax.random.uniform(next(ks), (DEPTH, 2, LRU_W), F32, 0.9, 0.999)
    sig = a_c ** (1.0 / LRU_C)
    return {
        'x_prompt': nrm((BATCH, SEQ, D_MODEL), 1.0),
        'x_sample': nrm((DEC_BATCH, DEC_SEQ, D_MODEL), 1.0),
        'cache_k': nrm((DEC_BATCH, DEPTH, PAST_LEN, N_KV_HEADS, HEAD_DIM), 1.0),
        'cache_v': nrm((DEC_BATCH, DEPTH, PAST_LEN, N_KV_HEADS, HEAD_DIM), 1.0),
        'state_lru': nrm((DEC_BATCH, DEPTH, 2, LRU_W), 0.5),
        'c': nrm((DEC_BATCH, D_MODEL), 1.0),
        'c_ctx': nrm((D_MODEL,), 1.0),
        'norm_g': 1.0 + nrm((DEPTH, D_MODEL), 0.02),
        'w_ada': nrm((DEPTH, D_MODEL, 3 * D_MODEL), D_MODEL ** -0.5),
        'b_ada': nrm((DEPTH, 3 * D_MODEL), 0.02),
        'w_in': nrm((DEPTH, D_MODEL, D_PROJ), D_MODEL ** -0.5),
        'q_norm_g': 1.0 + nrm((DEPTH, HEAD_DIM), 0.02),
        'k_norm_g': 1.0 + nrm((DEPTH, HEAD_DIM), 0.02),
        'hy_short_w': nrm((DEPTH, HY_SHORT, 3 * HY_W), HY_SHORT ** -0.5),
        'hy_short_b': nrm((DEPTH, 3 * HY_W), 0.02),
        'hy_filt_w1': nrm((DEPTH, HY_EMB, HY_FH), HY_EMB ** -0.5),
        'hy_filt_b1': nrm((DEPTH, HY_FH), 0.02),
        'hy_filt_w2': nrm((DEPTH, HY_FH, HY_FH), HY_FH ** -0.5),
        'hy_filt_b2': nrm((DEPTH, HY_FH), 0.02),
        'hy_filt_w3': nrm((DEPTH, HY_FH, 2 * HY_W), HY_FH ** -0.5),
        'hy_filt_freq': 1.0 + nrm((DEPTH, 2, HY_FH), 0.02),
        'hy_filt_decay': jax.random.uniform(next(ks), (DEPTH, 2 * HY_W), F32, HY_DECAY_MIN, HY_DECAY_MAX),
        'hy_bias': nrm((DEPTH, HY_W), 0.1),
        'lru_conv_w': nrm((DEPTH, LRU_CONV, LRU_W), LRU_CONV ** -0.5),
        'lru_conv_b': nrm((DEPTH, LRU_W), 0.02),
        'lru_wa': nrm((DEPTH, 2, LRU_BLOCKS, LRU_BS, LRU_BS), LRU_BS ** -0.5),
        'lru_ba': nrm((DEPTH, 2, LRU_W), 0.02),
        'lru_wx': nrm((DEPTH, 2, LRU_BLOCKS, LRU_BS, LRU_BS), LRU_BS ** -0.5),
        'lru_bx': nrm((DEPTH, 2, LRU_W), 0.02),
        'lru_lambda': jnp.log(sig) - jnp.log1p(-sig),
        'w_out': nrm((DEPTH, D_MIX, D_MODEL), D_MIX ** -0.5),
        'final_g': 1.0 + nrm((D_MODEL,), 0.02),
    }


def reference(x_prompt, x_sample, cache_k, cache_v, state_lru, c, c_ctx, norm_g, w_ada, b_ada, w_in,
              q_norm_g, k_norm_g, hy_short_w, hy_short_b, hy_filt_w1, hy_filt_b1, hy_filt_w2, hy_filt_b2,
              hy_filt_w3, hy_filt_freq, hy_filt_decay, hy_bias, lru_conv_w, lru_conv_b, lru_wa, lru_ba,
              lru_wx, lru_bx, lru_lambda, w_out, final_g):
    def layer_params(l):
        return {
            'norm_g': norm_g[l], 'w_ada': w_ada[l], 'b_ada': b_ada[l], 'w_in': w_in[l],
            'q_norm_g': q_norm_g[l], 'k_norm_g': k_norm_g[l],
            'hy_short_w': hy_short_w[l], 'hy_short_b': hy_short_b[l],
            'hy_filt_w1': hy_filt_w1[l], 'hy_filt_b1': hy_filt_b1[l],
            'hy_filt_w2': hy_filt_w2[l], 'hy_filt_b2': hy_filt_b2[l],
            'hy_filt_w3': hy_filt_w3[l], 'hy_filt_freq': hy_filt_freq[l],
            'hy_filt_decay': hy_filt_decay[l], 'hy_bias': hy_bias[l],
            'lru_conv_w': lru_conv_w[l], 'lru_conv_b': lru_conv_b[l],
            'lru_wa': lru_wa[l], 'lru_ba': lru_ba[l], 'lru_wx': lru_wx[l], 'lru_bx': lru_bx[l],
            'lru_lambda': lru_lambda[l], 'w_out': w_out[l],
        }

    y_p = x_prompt
    k_list, v_list, h_list = [], [], []
    for l in range(DEPTH):
        y_p, k_l, v_l, h_l = mixer_layer(y_p, c_ctx[None, :], layer_params(l))
        k_list.append(k_l)
        v_list.append(v_l)
        h_list.append(h_l)
    y_prompt = rmsnorm(y_p, final_g)
    new_k = jnp.stack(k_list, axis=1)
    new_v = jnp.stack(v_list, axis=1)
    new_lru = jnp.stack(h_list, axis=1)

    y_s = x_sample
    for l in range(DEPTH):
        y_s, _, _, _ = mixer_layer(y_s, c, layer_params(l), cache_k[:, l], cache_v[:, l], state_lru[:, l])
    y_sample = rmsnorm(y_s, final_g)
    return (y_prompt, y_sample, new_k, new_v, new_lru)
```

```python
import math
import os
import numpy as np
import ml_dtypes
import concourse.bass as bass
import concourse.mybir as mybir
from concourse.bass_utils import run_bass_kernel_spmd

F32 = mybir.dt.float32
BF16 = mybir.dt.bfloat16
I32 = mybir.dt.int32
ALU = mybir.AluOpType
AF = mybir.ActivationFunctionType

COMPUTE = ("pe", "act", "dve", "pool")
NDMASEM = 12
NCORES = 8
D = 1024
EPS = 1e-6
NG = 9


STOP = float(os.environ.get("KSTOP", "99"))


class _StopBuild(Exception):
    pass


def CK(n):
    if STOP <= n:
        raise _StopBuild()


class Sched:
    def __init__(self, nc):
        self.nc = nc
        self.ops = []
        self.last_writer = {}
        self.readers = {}
        self.alias = {}
        self.regst = {}

    def _summ(self, cur):
        last = {}
        out = []
        for i in cur:
            o = self.ops[i]
            if o["dma"]:
                out.append(i)
            else:
                last[o["eng"]] = i
        return out + list(last.values())

    def op(self, eng, emit, reads=(), writes=(), dma=False, cc=False):
        idx = len(self.ops)
        deps = {}
        regs = set()
        for k in list(reads) + list(writes):
            n = k[0] if isinstance(k, tuple) else k
            if n in self.alias:
                regs.add(self.alias[n])
        for region, owner in regs:
            st = self.regst.setdefault(region, dict(owner=None, cur=[], prev=[]))
            if st["owner"] != owner:
                st["prev"] = self._summ(st["cur"])
                st["cur"] = []
                st["owner"] = owner
            for i in st["prev"]:
                deps[i] = True
            st["cur"].append(idx)
        for k in reads:
            w = self.last_writer.get(k)
            if w is not None:
                deps[w] = True
        for k in writes:
            w = self.last_writer.get(k)
            if w is not None:
                deps.setdefault(w, False)
            seen = set()
            for r in reversed(self.readers.get(k, ())):
                if r == idx:
                    continue
                ro = self.ops[r]
                if not ro["dma"]:
                    if ro["eng"] in seen:
                        continue
                    seen.add(ro["eng"])
                deps.setdefault(r, False)
        for k in reads:
            self.readers.setdefault(k, []).append(idx)
        for k in writes:
            self.last_writer[k] = idx
            self.readers[k] = []
        self.ops.append(dict(eng=eng, emit=emit, deps=deps, dma=dma or cc, idx=idx, cc=cc))
        return idx

    def finalize(self):
        nc = self.nc
        ops = self.ops

        def skip_edge(o, do, raw):
            if do["dma"] or o["dma"]:
                return False
            if do["eng"] == o["eng"]:
                return o["eng"] == "pe" or not raw
            return False

        needed = set()
        for o in ops:
            for d, raw in o["deps"].items():
                if not skip_edge(o, ops[d], raw):
                    needed.add(d)
        sems = {e: nc.alloc_semaphore("sem_" + e) for e in COMPUTE}
        dsems = {q: [nc.alloc_semaphore("dsem_%s_%d" % (q, i)) for i in range(NDMASEM)] for q in ("sp", "pool")}
        tick = {e: 0 for e in COMPUTE}
        dcount = {q: 0 for q in dsems}
        duse = {q: [0] * NDMASEM for q in dsems}
        ccsem = nc.alloc_semaphore("ccsem")
        ccn = 0
        for o in ops:
            if o["cc"]:
                ccn += 1
                o["sem"] = ccsem
                o["val"] = ccn
                o["prev"] = 0
            elif o["dma"]:
                q = o["eng"]
                s = dcount[q] % NDMASEM
                dcount[q] += 1
                duse[q][s] += 1
                o["sem"] = dsems[q][s]
                o["val"] = 16 * duse[q][s]
                o["prev"] = 16 * (duse[q][s] - 1)
            else:
                e = o["eng"]
                o["inc"] = o["idx"] in needed
                if o["inc"]:
                    tick[e] += 1
                o["sem"] = sems[e]
                o["val"] = tick[e]
        streams = {e: [] for e in ("pe", "act", "dve", "pool", "sp")}
        for o in ops:
            streams[o["eng"]].append(o)
        final_dma = [o for o in ops if o["dma"] and not o["cc"]]

        def run_stream(ename, eng):
            waited = {}
            for o in streams[ename]:
                waits = {}
                for d, raw in o["deps"].items():
                    do = ops[d]
                    if skip_edge(o, do, raw):
                        continue
                    key = id(do["sem"])
                    if waits.get(key, (None, -1))[1] < do["val"]:
                        waits[key] = (do["sem"], do["val"])
                if o["dma"] and o["prev"] > 0:
                    key = id(o["sem"])
                    if waits.get(key, (None, -1))[1] < o["prev"]:
                        waits[key] = (o["sem"], o["prev"])
                for key, (s, v) in waits.items():
                    if waited.get(key, -1) >= v:
                        continue
                    eng.wait_ge(s, v)
                    waited[key] = v
                ins = o["emit"](eng)
                if o["cc"]:
                    ins.then_inc(o["sem"], 1)
                elif o["dma"]:
                    ins.then_inc(o["sem"], 16)
                elif o["inc"]:
                    ins.then_inc(o["sem"], 1)
            if ename == "sp":
                last = {}
                for o in final_dma:
                    k = id(o["sem"])
                    last[k] = (o["sem"], max(o["val"], last.get(k, (None, 0))[1]))
                for k, (s, v) in last.items():
                    eng.wait_ge(s, v)

        with nc.Block() as block:
            @block.tensor
            def _(e):
                run_stream("pe", e)

            @block.scalar
            def _(e):
                run_stream("act", e)

            @block.vector
            def _(e):
                run_stream("dve", e)

            @block.gpsimd
            def _(e):
                run_stream("pool", e)

            @block.sync
            def _(e):
                run_stream("sp", e)
        return dict(n_ops=len(ops), ticks=dict(tick), dmas=dict(dcount))


PC_NG = 0
PC_HSW = 8
PC_HSB = 44
PC_HYB = 56
PC_LCW = 60
PC_LCB = 76
PC_LBA = 80
PC_LBX = 88
PC_LAM = 96
PC_QG = 104
PC_KG = 105
PC_B1 = 106
PC_B2 = 107
PC_F0 = 108
PC_F1 = 109
PC_N = 112


def build_program():
    nc = bass.Bass("TRN2", target_bir_lowering=False)

    def din(name, shape, dt=F32):
        return nc.dram_tensor(name, list(shape), dt, kind="ExternalInput").ap()

    def dout(name, shape):
        return nc.dram_tensor(name, list(shape), F32, kind="ExternalOutput").ap()

    xg_d = [din("xp", [1024, D]), din("xs", [1024, D])]
    ckT_d = din("ckT", [2, 64, 512])
    cv_d = din("cv", [2, 512, 64])
    wins_d = din("w_in_s", [2, D, 3 * 512])
    wouts_d = din("w_out_s", [2, 1536, D])
    pps_d = din("pps", [2, 128, PC_N])
    bds_d = din("bds", [2, 128, 16 * 128])
    w3ss_d = din("w3ss", [2, 64, 256])
    decs_d = din("decs", [2, 256])
    ccin_d = [[nc.dram_tensor("ccin%d_%d" % (i, t), [128, 1024], BF16) for t in range(3)] for i in range(2)]
    ccout_d = [[nc.dram_tensor("ccout%d_%d" % (i, t), [512, 1024], BF16) for t in range(3)] for i in range(2)]
    st_d = din("st", [128, 16])
    cp_d = din("cpack", [128, 16])
    wada_d = din("w_ada", [2, D, 3072])
    bada_d = din("b_ada", [2, 3072])
    win_d = din("w_in_r", [2, D, NG * 512])
    wout_d = din("w_out", [2, 1536, D])
    pp_d = din("ppack", [2, 128, PC_N])
    bd_d = din("lru_bd", [2, 128, 16 * 128])
    w1_d = din("w1", [2, 33, 64])
    w2_d = din("w2", [2, 64, 64])
    w3_d = din("w3r", [2, 64, 1024])
    dec_d = din("decr", [2, 1024])
    fg_d = din("final_g", [D])
    ident_d = din("ident", [128, 128], BF16)
    psw_d = din("psw", [128, 128], BF16)
    bones_d = din("bones", [128, 128], BF16)
    ft_d = {256: din("ft256", [33, 256]), 1024: din("ft1024", [33, 1024])}
    tc_d = {256: din("tc256", [128, 2]), 1024: din("tc1024", [128, 8])}
    fwd_d = {256: din("fwd256", [256, 512], BF16), 1024: din("fwd1024", [1024, 2048], BF16)}
    inv_d = {256: din("inv256", [512, 256], BF16), 1024: din("inv1024", [2048, 1024], BF16)}
    cos_d = din("ropec", [128, 1024], BF16)
    sin_d = din("ropes", [128, 1024], BF16)

    yg_d = [dout("yp", [1024, D]), dout("ys", [1024, D])]
    nk_d = dout("nk", [2, 1024, 128])
    nv_d = dout("nv", [2, 1024, 128])
    nl_d = dout("nl", [2, 8, 512])

    S = Sched(nc)

    def sb(name, shape, dt=F32):
        return nc.alloc_sbuf_tensor("s_" + name, list(shape), dt)

    x_t = sb("x_t", [128, 8, D])
    hT = sb("hT", [128, 8, 1024], BF16)
    wbuf = [sb("wbuf%d" % i, [128, 8, 512], BF16) for i in range(2)]
    mixT = sb("mixT", [128, 12, 1024], BF16)
    gate_bc = sb("gate_bc", [128, D])
    fg_bc = gate_bc
    gaterow = sb("gaterow", [33, 2, 1024], BF16)
    ABall = sb("ABall", [128, 4, 16])
    ident = sb("ident", [128, 128], BF16)
    psw = sb("psw", [128, 128], BF16)
    bones = sb("bones", [128, 128], BF16)
    onesb = sb("onesb", [128, 128], BF16)
    onesf = sb("onesf", [128, 128])
    pp = sb("pp", [128, PC_N])
    small = sb("small", [128, 64])
    ssq = sb("ssq", [128, 8])
    rstd = sb("rstd", [128, 8])
    csb = sb("csb", [128, 8, 64], BF16)
    cpk = sb("cpk", [128, 16])
    h0t = sb("h0t", [128, 16])
    bd = sb("bd", [128, 16, 128], BF16)
    w1s = sb("w1s", [33, 64])
    w2s = sb("w2s", [64, 64])
    w3s = sb("w3s", [64, 512])
    decbc = sb("decbc", [128, 512])
    tcol = sb("tcol", [128, 8])
    ntcol = sb("ntcol", [128, 8])
    ar_a = sb("ar_a", [128, 8192])
    ar_b = sb("ar_b", [128, 4096])
    ar_c = sb("ar_c", [128, 4096])
    ar_d = sb("ar_d", [128, 3072])
    PT = ar_a[:, 0:6144].bitcast(BF16).rearrange("p (s k n) -> p s k n", s=2, k=12)
    qT = ar_b[:, 0:2048].bitcast(BF16).rearrange("p (c n) -> p c n", c=4)
    kkT = ar_b[:, 2048:3584].bitcast(BF16).rearrange("p (g n) -> p g n", g=2)
    vv = ar_c[:, 0:1536].bitcast(BF16).rearrange("p (t n) -> p t n", t=12)
    sga = ar_c[:, 1536:3584].bitcast(BF16).rearrange("p (c n) -> p c n", c=4)
    ko = ar_d[:, 0:1024].rearrange("p (t n) -> p t n", t=8)
    nvo = ar_d[:, 1024:2048].rearrange("p (t n) -> p t n", t=8)
    rcp = ar_d[:, 2048:2560]
    ytmp = ar_d[:, 2560:3072]
    xc = ar_a[:, 0:4096].rearrange("p (c n) -> p c n", c=4)
    xcb = ar_a[:, 4096:6144].bitcast(BF16).rearrange("p (c n) -> p c n", c=4)
    avs = [ar_b[:, 0:1024], ar_a[:, 6144:7168]]
    bvs = [ar_b[:, 1024:2048], ar_a[:, 7168:8192]]
    hsf = ar_b[:, 2048:3072]
    rt = ar_b[:, 3072:4096]
    it = ar_d[:, 2048:3072]
    yacc = ar_c[:, 0:4096].rearrange("p (c n) -> p c n", c=4)
    sgl = ar_d[:, 0:2048].bitcast(BF16).rearrange("p (c n) -> p c n", c=4)
    HL = small[:, 24:56].rearrange("p (c q) -> p c q", c=4)
    dft = ar_a[:, :].bitcast(BF16)
    x0c = ar_b[:, 0:2048].rearrange("p (c n) -> p c n", c=2)
    zT = ar_b[:, 2048:3072].bitcast(BF16).rearrange("p (c n) -> p c n", c=2)
    ztok = ar_b[:, 3072:4096].bitcast(BF16).rearrange("p (t n) -> p t n", t=8)
    Ypk = ar_c[:, 0:2048].bitcast(BF16)
    filt = ar_c[:, 2048:4096].bitcast(BF16).rearrange("p (t n) -> p t n", t=8)
    sgh = ar_d[:, 0:1024].bitcast(BF16).rearrange("p (c n) -> p c n", c=2)
    Hs = ar_d[:, 1024:1536].rearrange("p (a n) -> p a n", a=2)
    Zs = ar_d[:, 1536:2048]
    rnb = ar_d[:, 2048:2304]
    hyt1 = ar_d[:, 2304:2560]
    hyt2 = ar_d[:, 2560:2816]
    hyt3 = ar_d[:, 2816:3072]
    ust = sb("ust", [128, 2, 1024])
    cvt = ar_c[:, 0:1024]
    xn = ust[:, :, :].rearrange("p a n -> p (a n)").bitcast(BF16).rearrange("p (j n) -> p j n", j=4)
    XSW = 1056
    xsb = [ust[:, 0, 0:528].bitcast(BF16), ust[:, 0, 528:1024].bitcast(BF16)[:, 0:0] if False else ust[:, 1, 0:528].bitcast(BF16)]
    dgt = ust[:, 1, 528:784].bitcast(BF16).rearrange("p (j c) -> p j c", j=4)
    ft = ust[0:33, 1, :]
    sc1 = sb("sc1", [128, 512])
    sc2 = sb("sc2", [128, 512])
    sc3 = sb("sc3", [128, 512])
    sqb = sc3[:, 0:256].bitcast(BF16)
    qnb = sb("qnb", [128, 512], BF16)
    hdn1 = sb("hdn1", [64, 1024])
    tA = sc1[0:64, :]
    tI = sc2[0:64, :].bitcast(I32)
    tF = sc3[0:64, :]
    wada = wbuf
    wo = [wbuf[i][:, :, :].rearrange("p k c -> p (k c)")[:, 0:3072].rearrange("p (k c) -> p k c", k=12) for i in range(2)]
    modtmp = ar_c[0:33, 0:3072]
    for n_, ro in (("badab", ("ar_a", "ada")), ("modtmp", ("ar_c", "ada")),
                   ("PT", ("ar_a", "attn")), ("xc", ("ar_a", "lru")), ("xcb", ("ar_a", "lru")), ("dft", ("ar_a", "hy")),
                   ("qT", ("ar_b", "attn")), ("kk", ("ar_b", "attn")),
                   ("av", ("ar_b", "lru")), ("bv", ("ar_b", "lru")), ("avx", ("ar_a", "lru")), ("hsf", ("ar_b", "lru")), ("rt", ("ar_b", "lru")), ("it", ("ar_d", "lru")),
                   ("x0c", ("ar_b", "hy")), ("zT", ("ar_b", "hy")), ("ztok", ("ar_b", "hy")),
                   ("vv", ("ar_c", "attn")), ("sga", ("ar_c", "attn")), ("yacc", ("ar_c", "lru")), ("Y", ("ar_c", "hy")), ("filt", ("ar_c", "hy")),
                   ("ko", ("ar_d", "attn")), ("nvo", ("ar_d", "attn")), ("rcp", ("ar_d", "attn")), ("ytmp", ("ar_d", "attn")),
                   ("sgl", ("ar_d", "lru")),
                   ("sgh", ("ar_d", "hy")), ("Hs", ("ar_d", "hy")), ("Zs", ("ar_d", "hy")), ("rnb", ("ar_d", "hy")),
                   ("hyt1", ("ar_d", "hy")), ("hyt2", ("ar_d", "hy")), ("hyt3", ("ar_d", "hy")),
                   ("xn", ("r_ust", "xn")), ("ust", ("r_ust", "ust"))):
        S.alias[n_] = ro

    banks = [nc.alloc_psum_tensor("pb%d" % i, [128, 512], F32) for i in range(4)]
    pst = nc.alloc_psum_tensor("pst", [128, 1024], F32)
    pab = nc.alloc_psum_tensor("pab", [128, 1024], F32)
    banks = [bk[:, :] for bk in banks] + [pst[:, 0:512], pst[:, 512:1024], pab[:, 0:512], pab[:, 512:1024]]
    rot = {"mm": [0, 1, 2], "tr": [3, 7], "st": [4, 5], "A": [6, 0], "B": [7, 1]}
    rpos = {k: 0 for k in rot}

    def bank(cls):
        b = rot[cls][rpos[cls] % len(rot[cls])]
        rpos[cls] += 1
        return b

    def PB(b):
        return banks[b]

    def MM(out, lhsT, rhs, start, stop, reads, writes):
        S.op("pe", lambda e: e.matmul(out, lhsT=lhsT, rhs=rhs, start=start, stop=stop), reads, writes)

    def TR(out, in_, reads, writes):
        S.op("pe", lambda e: e.transpose(out=out, in_=in_, identity=ident[:]), list(reads) + ["ident"], writes)

    def ACT(out, in_, func, reads, writes, scale=1.0, bias=None, accum=None):
        kw = {}
        if bias is not None:
            kw["bias"] = bias
        if accum is not None:
            kw["accum_out"] = accum
        S.op("act", lambda e: e.activation(out=out, in_=in_, func=func, scale=scale, **kw), reads, writes)

    def TS(eng, out, in0, s1, s2, op0, op1, reads, writes):
        if s2 is None:
            S.op(eng, lambda e: e.tensor_scalar(out=out, in0=in0, scalar1=s1, scalar2=None, op0=op0), reads, writes)
        else:
            S.op(eng, lambda e: e.tensor_scalar(out=out, in0=in0, scalar1=s1, scalar2=s2, op0=op0, op1=op1), reads, writes)

    def TT(eng, out, in0, in1, op, reads, writes):
        S.op(eng, lambda e: e.tensor_tensor(out=out, in0=in0, in1=in1, op=op), reads, writes)

    def STT(eng, out, in0, scalar, in1, op0, op1, reads, writes):
        eng = "dve"
        S.op(eng, lambda e: e.scalar_tensor_tensor(out=out, in0=in0, scalar=scalar, in1=in1, op0=op0, op1=op1), reads, writes)

    def CP(eng, out, in_, reads, writes):
        if eng == "act":
            S.op(eng, lambda e: e.activation(out=out, in_=in_, func=AF.Identity), reads, writes)
        else:
            S.op(eng, lambda e: e.tensor_copy(out=out, in_=in_), reads, writes)

    def MEMSET(eng, ap, val, writes):
        S.op(eng, lambda e: e.memset(ap, val), (), writes)

    def DMA(out, in_, reads, writes, q="sp"):
        S.op(q, lambda e: e.dma_start(out=out, in_=in_), reads, writes, dma=True)

    epsc = small[:, 0:1]
    onec = small[:, 1:2]
    lnhalf = small[:, 2:3]
    badab = ar_a[0:33, 0:3072]
    def startup():
        DMA(ident[:], ident_d, (), ["ident"])
        DMA(psw[:], psw_d, (), ["psw"])
        DMA(bones[:], bones_d, (), ["bones"])
        DMA(cpk[:], cp_d, (), ["cpk"])
        DMA(h0t[:], st_d, (), ["h0t"])
        MEMSET("dve", onesb[:], 1.0, ["onesb"])
        MEMSET("dve", onesf[:], 1.0, ["onesf"])
        MEMSET("dve", csb[:], 0.0, ["csb"])
        MEMSET("dve", small[:], 0.0, ["small"])
        MEMSET("dve", small[:, 0:1], EPS, ["small"])
        MEMSET("dve", small[:, 1:2], 1.0, ["small"])
        MEMSET("dve", small[:, 2:3], math.log(0.5), ["small"])

        CK(-5)
        ACT(csb[:, :, ::32], cpk[:, :].rearrange("p (k v) -> p k v", v=2), AF.Silu, ["cpk", "csb"], ["csb"])
        CK(-4)
        DMA(pp[:], pp_d[0], (), ["pp"])
        for l in range(2):
            DMA(badab, bada_d[l].partition_broadcast(33), (), ["badab"])
            if l == 1:
                DMA(pp[:], pp_d[1], (), ["pp"])
            for j in range(6):
                slot = j % 2
                DMA(wada[slot][:], wada_d[l].rearrange("(k p) c -> p k c", p=128)[:, :, j * 512:(j + 1) * 512], (), [("win", slot)], q="pool")
                b = bank("mm")
                for kc in range(8):
                    MM(PB(b)[0:33, :], csb[:, kc, 0:33], wada[slot][:, kc, :], kc == 0, kc == 7,
                       [("win", slot), "csb"], [("ps", b)])
                TT("dve", modtmp[:, j * 512:(j + 1) * 512], PB(b)[0:33, :], badab[:, j * 512:(j + 1) * 512], ALU.add,
                   [("ps", b), "badab"], ["modtmp"])
            CK(-30 + l)
            CP("dve", gaterow[:, l, :], modtmp[:, 2048:3072], ["modtmp"], [("gaterow", l)])
            CK(-20 + l)
            for gi in range(2):
                r0 = 32 * gi
                b = bank("mm")
                for cc in range(16):
                    MM(PB(b)[:, cc:cc + 1], modtmp[r0:r0 + 1, cc * 128:(cc + 1) * 128], onesf[r0:r0 + 1, 0:1], True, True,
                       ["modtmp", "onesf"], [("ps", b)])
                ABv = ABall[:, gi * 2 + l, :]
                CP("dve", ABv[:, 8:16], PB(b)[:, 0:8], [("ps", b)], [("AB", gi, l)])
                STT("dve", ABv[:, 0:8], PB(b)[:, 8:16], 1.0, pp[:, PC_NG:PC_NG + 8], ALU.add, ALU.mult, [("ps", b), "pp"], [("AB", gi, l)])

    def layer_pass(gi, l, nseq, L):
        latent = gi == 1
        r0 = 32 * gi
        NQC, NKV, NLC, HH, NC2 = (4, 2, 4, 2, 2) if gi == 0 else (1, 1, 1, 1, 1)
        CW = NC2 * 128
        VW = NKV * 128
        win_src = win_d if gi == 0 else wins_d
        NGP = 9 if gi == 0 else 3
        if gi == 0:
            LOC = dict(q=lambda cc: (0, cc * 128), kk=lambda g: (1, g * 128), v=(1, 256), ga=lambda cc: (2, cc * 128),
                       lx=lambda cc: (3, cc * 128), lg=lambda cc: (4, cc * 128),
                       x0=lambda hh, c2: (5 + 2 * hh, c2 * 128), x1=lambda hh, c2: (5 + 2 * hh, 256 + c2 * 128),
                       hv=lambda hh, c2: (6 + 2 * hh, c2 * 128), hg=lambda hh, c2: (6 + 2 * hh, 256 + c2 * 128))
            PF = dict(start=[0, 1], after_q=[2], after_k=[3], lru_start=[4], after_lx=[5], hh0=[6], after_hx0=[7], hh1=[8])
        else:
            LOC = dict(q=lambda cc: (0, 0), kk=lambda g: (0, 128), v=(0, 256), ga=lambda cc: (0, 384),
                       lx=lambda cc: (1, 0), lg=lambda cc: (1, 128), x0=lambda hh, c2: (1, 256), x1=lambda hh, c2: (1, 384),
                       hv=lambda hh, c2: (2, 0), hg=lambda hh, c2: (2, 128))
            PF = dict(start=[0, 1], lru_start=[2])
        NQ = min(L, 512)
        nqb = L // NQ
        ntl = L // 128
        Lctx = 512 if latent else 0
        nkt = (Lctx + L) // 128
        nf = L // 128

        DMA(pp[:], (pp_d if gi == 0 else pps_d)[l], (), ["pp"])
        DMA(bd[:], (bd_d if gi == 0 else bds_d)[l].rearrange("p (n c) -> p n c", n=16), (), ["bd"], q="pool")
        DMA(w1s[:], w1_d[l], (), ["w1s"])
        DMA(w2s[:], w2_d[l], (), ["w2s"])
        DMA(tcol[:, 0:nf], tc_d[L], (), ["tcol"])
        TS("dve", ntcol[:, 0:nf], tcol[:, 0:nf], -1.0, None, ALU.mult, None, ["tcol"], ["ntcol"])

        AB = ABall[:, gi * 2 + l, :]
        for h in range(2):
            b = bank("mm")
            MM(PB(b)[:, :], onesb[r0:r0 + 1, 0:128], gaterow[r0:r0 + 1, l, h * 512:(h + 1) * 512], True, True,
               [("gaterow", l), "onesb"], [("ps", b)])
            CP("act", gate_bc[:, h * 512:(h + 1) * 512], PB(b)[:, :], [("ps", b)], [("gate", h)])

        for hb in range(2):
            for jj in range(4):
                j = hb * 4 + jj
                ACT(xn[:, jj, :], x_t[:, j, :], AF.Square, [("x", j)], [("xn", jj), ("ssq", hb)], accum=ssq[:, j:j + 1])
            ACT(rstd[:, hb * 4:hb * 4 + 4], ssq[:, hb * 4:hb * 4 + 4], AF.Ln, [("ssq", hb), "small"], [("rstd", hb)], scale=1.0 / D, bias=epsc)
            ACT(rstd[:, hb * 4:hb * 4 + 4], rstd[:, hb * 4:hb * 4 + 4], AF.Exp, [("rstd", hb)], [("rstd", hb)], scale=-0.5)
            for jj in range(4):
                j = hb * 4 + jj
                TS("dve", xn[:, jj, :], x_t[:, j, :], rstd[:, j:j + 1], None, ALU.mult, None, [("x", j), ("rstd", hb)], [("xn", jj)])
            for ccp in range(4):
                b = bank("tr")
                pbv = PB(b)[:, :].bitcast(BF16).rearrange("p (c n) -> p c n", c=2)
                for ci in range(2):
                    cc = ccp * 2 + ci
                    for jj in range(4):
                        TR(pbv[:, ci, jj * 128:(jj + 1) * 128], xn[:, jj, cc * 128:(cc + 1) * 128], [("xn", jj)], [("ps", b)])
                for ci in range(2):
                    cc = ccp * 2 + ci
                    TS("dve", hT[:, cc, hb * 512:(hb + 1) * 512], pbv[:, ci, :], AB[:, cc:cc + 1], AB[:, 8 + cc:9 + cc], ALU.mult, ALU.add,
                       [("ps", b), ("AB", gi, l)], [("hT", cc, hb)])

        CK(2 + 10 * (gi * 2 + l))
        loaded = set()

        def ensure(g):
            if g in loaded or g >= NGP:
                return
            loaded.add(g)
            slot = g % 2
            DMA(wbuf[slot][:], win_src[l].rearrange("(k p) c -> p k c", p=128)[:, :, g * 512:(g + 1) * 512], (), [("win", slot)], q="pool")

        def pf(tag):
            for g_ in PF.get(tag, []):
                ensure(g_)

        def inproj(loc, hf):
            g_, c0 = loc
            ensure(g_)
            slot = g_ % 2
            b = bank("mm")
            for kc in range(8):
                MM(PB(b)[:, :], wbuf[slot][:, kc, c0:c0 + 128], hT[:, kc, hf * 512:(hf + 1) * 512], kc == 0, kc == 7,
                   [("win", slot), ("hT", kc, hf)], [("ps", b)])
            return b

        def headnorm(b, gcol, dst, dkeys, hf):
            ACT(sqb, PB(b)[:, :], AF.Square, [("ps", b)], ["sc3"])
            bs = bank("st")
            MM(PB(bs)[:, :], bones[:], sqb, True, True, ["bones", "sc3"], [("ps", bs)])
            ACT(sc1[:], PB(bs)[:, :], AF.Ln, [("ps", bs), "small"], ["sc1"], scale=1.0 / 64, bias=epsc)
            ACT(sc1[:], sc1[:], AF.Exp, ["sc1"], ["sc1"], scale=-0.5)
            if not latent:
                STT("dve", dst, PB(b)[:, :], pp[:, gcol:gcol + 1], sc1[:], ALU.mult, ALU.mult, [("ps", b), "pp", "sc1"], dkeys)
            else:
                STT("dve", qnb[:], PB(b)[:, :], pp[:, gcol:gcol + 1], sc1[:], ALU.mult, ALU.mult, [("ps", b), "pp", "sc1"], ["qnb"])
                bs2 = bank("st")
                MM(PB(bs2)[:, :], psw[:], qnb[:], True, True, ["psw", "qnb"], [("ps", bs2)])
                TT("dve", sc2[:], qnb[:], ropec[:, hf * 512:(hf + 1) * 512], ALU.mult, ["qnb", "ropec"], ["sc2"])
                TT("dve", sc3[:], PB(bs2)[:, :], ropes[:, hf * 512:(hf + 1) * 512], ALU.mult, [("ps", bs2), "ropes"], ["sc3"])
                TT("pool", dst, sc2[:], sc3[:], ALU.add, ["sc2", "sc3"], dkeys)

        def gather(t, lslot):
            lk = [("mix", lslot, 0), ("mix", lslot, 1)]
            ok = [("mix", 4 * t + r_, h_) for r_ in range(4) for h_ in range(2)]
            DMA(ccin_d[l][t].ap(), mixT[:, lslot, :], lk, [("ccin", l, t)])
            S.op("pool", (lambda l_, t_: lambda e: e.collective_compute("AllGather", ALU.bypass, replica_groups=[[0, 2, 4, 6], [1, 3, 5, 7]],
                                                                         ins=[ccin_d[l_][t_].ap()], outs=[ccout_d[l_][t_].ap()]))(l, t),
                 [("ccin", l, t)], [("ccout", l, t)], cc=True)
            DMA(mixT[:, 4 * t:4 * t + 4, :], ccout_d[l][t].ap().rearrange("(c p) n -> p c n", p=128), [("ccout", l, t)], ok)

        xst = {"n": 0}
        Lp = L + 4

        def zero_pads():
            for xb in xsb:
                x3 = xb[:, 0:nseq * Lp].rearrange("p (s n) -> p s n", s=nseq)
                MEMSET("pool", x3[:, :, 0:1], 0.0, [("ust", 0), ("ust", 1)])
                MEMSET("pool", x3[:, :, L + 1:L + 4], 0.0, [("ust", 0), ("ust", 1)])

        def proj_conv(loc, ntaps, wcol, evac):
            slot = xst["n"] % 2
            xst["n"] += 1
            xb = xsb[slot]
            x3 = xb[:, 0:nseq * Lp].rearrange("p (s n) -> p s n", s=nseq)
            for hf in range(2):
                b = inproj(loc, hf)
                if L == 256:
                    CP("act", x3[:, 2 * hf:2 * hf + 2, 1:1 + L], PB(b)[:, :].rearrange("p (s n) -> p s n", s=2), [("ps", b)], [("ust", slot)])
                else:
                    CP("act", x3[:, 0, 1 + hf * 512:1 + (hf + 1) * 512], PB(b)[:, :], [("ps", b)], [("ust", slot)])
            for j in range(ntaps):
                TS("dve", dgt[:, j, :], ident[:], wcol(j), None, ALU.mult, None, ["ident", "pp", ("ust", 1)], ["dg"])
            for hf in range(2):
                b = bank("st")
                if L == 256:
                    for si in range(2):
                        sq = 2 * hf + si
                        for j in range(ntaps):
                            MM(PB(b)[:, si * 256:(si + 1) * 256], dgt[:, j, :], x3[:, sq, j:j + L], j == 0, j == ntaps - 1,
                               ["dg", ("ust", slot)], [("ps", b)])
                else:
                    for j in range(ntaps):
                        MM(PB(b)[:, :], dgt[:, j, :], x3[:, 0, hf * 512 + j:hf * 512 + j + 512], j == 0, j == ntaps - 1,
                           ["dg", ("ust", slot)], [("ps", b)])
                evac(hf, b)

        pf("start")
        if latent:
            for dup in range(2):
                DMA(kkT[dup * 64:(dup + 1) * 64, 0, 0:512], ckT_d[l], (), [("kk", 0, 0)], q="pool")
                DMA(vv[:, 0:4, dup * 64:(dup + 1) * 64], cv_d[l].rearrange("(t p) d -> p t d", p=128), (), [("vv", t) for t in range(4)], q="pool")
        for cc in range(NQC):
            for hf in range(2):
                b = inproj(LOC["q"](cc), hf)
                headnorm(b, PC_QG, qT[:, cc, hf * 512:(hf + 1) * 512], [("qT", cc, hf)], hf)
        CK(2.1 + 10 * (gi * 2 + l))
        pf("after_q")
        for g in range(NKV):
            for hf in range(2):
                b = inproj(LOC["kk"](g), hf)
                headnorm(b, PC_KG, kkT[:, g, Lctx + hf * 512:Lctx + (hf + 1) * 512], [("kk", g, 1 + hf)], hf)
        CK(2.2 + 10 * (gi * 2 + l))
        vt0 = Lctx // 128
        vg_, vcol = LOC["v"]
        ensure(vg_)
        s_k = vg_ % 2
        for j in range(8):
            b = bank("mm")
            for kc in range(8):
                MM(PB(b)[:, 0:VW], hT[:, kc, j * 128:(j + 1) * 128], wbuf[s_k][:, kc, vcol:vcol + VW], kc == 0, kc == 7,
                   [("win", s_k), ("hT", kc, j // 4)], [("ps", b)])
            CP("act", vv[:, vt0 + j, 0:VW], PB(b)[:, 0:VW], [("ps", b)], [("vv", vt0 + j)])
            if not latent and os.environ.get("KNVO", "1") == "1":
                CP("act", nvo[:, j, :].rearrange("p (k d) -> p k d", k=2),
                   PB(b)[:, 0:256].rearrange("p (k u d) -> p k u d", k=2, u=2)[:, :, 0, :], [("ps", b)], ["nvo"])
        CK(2.3 + 10 * (gi * 2 + l))
        if not latent:
            DMA(nv_d[l].rearrange("(t p) c -> p t c", p=128), nvo[:, :, :], ["nvo"], ())
            for j in range(8):
                b = bank("tr")
                pbv = PB(b)[:, :].bitcast(BF16)
                for g in range(2):
                    TR(pbv[:, g * 128:(g + 1) * 128], kkT[:, g, j * 128:(j + 1) * 128], [("kk", g, 1 + j // 4)], [("ps", b)])
                CP("act", ko[:, j, :].rearrange("p (g d) -> p g d", g=2),
                   pbv[:, 0:256].rearrange("p (g n) -> p g n", g=2)[:, :, 0:64], [("ps", b)], ["ko"])
            DMA(nk_d[l].rearrange("(t p) c -> p t c", p=128), ko[:, :, :], ["ko"], ())
        CK(2.4 + 10 * (gi * 2 + l))
        pf("after_k")
        for cc in range(NQC):
            for hf in range(2):
                b = inproj(LOC["ga"](cc), hf)
                ACT(sga[:, cc, hf * 512:(hf + 1) * 512], PB(b)[:, :], AF.Silu, [("ps", b)], [("sga", cc, hf)])
        CK(2.5 + 10 * (gi * 2 + l))
        i2p = 1.0 / (2 * math.pi)
        TS("dve", small[0:64, 16:17], pp[0:64, PC_F0:PC_F0 + 1], i2p, None, ALU.mult, None, ["pp", "small"], ["small"])
        TT("dve", small[0:64, 17:18], small[0:64, 16:17], pp[0:64, PC_B1:PC_B1 + 1], ALU.mult, ["pp", "small"], ["small"])
        TS("dve", small[0:64, 18:19], pp[0:64, PC_F1:PC_F1 + 1], i2p, None, ALU.mult, None, ["pp", "small"], ["small"])
        TT("dve", small[0:64, 19:20], small[0:64, 18:19], pp[0:64, PC_B2:PC_B2 + 1], ALU.mult, ["pp", "small"], ["small"])
        DMA(ft[:, 0:L], ft_d[L], (), [("ust", 1)])
        hdn0 = ust[0:64, 0, :]
        for stage in range(2):
            for blk in range(nqb):
                n0 = blk * NQ
                b = bank("mm")
                if stage == 0:
                    MM(PB(b)[0:64, 0:NQ], w1s[0:33, :], ft[0:33, n0:n0 + NQ], True, True, ["w1s", ("ust", 1)], [("ps", b)])
                else:
                    MM(PB(b)[0:64, 0:NQ], w2s[0:64, :], hdn0[:, n0:n0 + NQ], True, True, ["w2s", ("ust", 0)], [("ps", b)])
                c1 = 16 + stage * 2
                TS("dve", tA[:, 0:NQ], PB(b)[0:64, 0:NQ], small[0:64, c1:c1 + 1], small[0:64, c1 + 1:c1 + 2], ALU.mult, ALU.add, [("ps", b), "small"], ["sc1"])
                CP("dve", tI[:, 0:NQ], tA[:, 0:NQ], ["sc1"], ["sc2"])
                CP("dve", tF[:, 0:NQ], tI[:, 0:NQ], ["sc2"], ["sc3"])
                TT("dve", tA[:, 0:NQ], tA[:, 0:NQ], tF[:, 0:NQ], ALU.subtract, ["sc1", "sc3"], ["sc1"])
                if stage == 0:
                    ACT(hdn0[:, n0:n0 + NQ], tA[:, 0:NQ], AF.Sin, ["sc1"], [("ust", 0)], scale=2 * math.pi)
                else:
                    ACT(hdn1[:, n0:n0 + NQ], tA[:, 0:NQ], AF.Sin, ["sc1"], ["hdn1"], scale=2 * math.pi)

        items = [(s_, h_, qb_) for s_ in range(nseq) for h_ in range(2 * NQC) for qb_ in range(nqb)]

        def att_s1(k):
            s, h, qb = items[k]
            g = (h // 4) if NKV == 2 else 0
            cc = h // 2
            hp = (h % 2) * 64
            q0 = s * L + qb * NQ
            qhf = q0 // 512
            pslot = k % 2
            for kt in range(nkt):
                koff = (kt * 128) if latent else (s * L + kt * 128)
                kkey = ("kk", g, 0) if (latent and kt < 4) else ("kk", g, 1 + (koff - Lctx) // 512)
                bs = bank("st")
                MM(PB(bs)[:, 0:NQ], kkT[hp:hp + 64, g, koff:koff + 128], qT[hp:hp + 64, cc, q0:q0 + NQ], True, True,
                   [kkey, ("qT", cc, qhf)], [("ps", bs)])
                ACT(PT[:, pslot, kt, 0:NQ], PB(bs)[:, 0:NQ], AF.Exp, [("ps", bs)], [("PT", pslot, kt)], scale=0.125)

        def att_s2(k):
            s, h, qb = items[k]
            g = (h // 4) if NKV == 2 else 0
            cc = h // 2
            hp = (h % 2) * 64
            q0 = s * L + qb * NQ
            qhf = q0 // 512
            pslot = k % 2
            bA = bank("A")
            bB = bank("B")
            for kt in range(nkt):
                vt = kt if latent else (s * ntl + kt)
                MM(PB(bA)[:, 0:NQ], vv[:, vt, g * 128:(g + 1) * 128], PT[:, pslot, kt, 0:NQ], kt == 0, kt == nkt - 1,
                   [("vv", vt), ("PT", pslot, kt)], [("ps", bA)])
            for kt in range(nkt):
                MM(PB(bB)[:, 0:NQ], onesb[:], PT[:, pslot, kt, 0:NQ], kt == 0, kt == nkt - 1,
                   ["onesb", ("PT", pslot, kt)], [("ps", bB)])
            ACT(rcp[hp:hp + 64, 0:NQ], PB(bB)[hp:hp + 64, 0:NQ], AF.Ln, [("ps", bB)], [("rcp", hp)])
            ACT(rcp[hp:hp + 64, 0:NQ], rcp[hp:hp + 64, 0:NQ], AF.Exp, [("rcp", hp)], [("rcp", hp)], scale=-1.0)
            TT("dve", ytmp[hp:hp + 64, 0:NQ], PB(bA)[hp:hp + 64, 0:NQ], rcp[hp:hp + 64, 0:NQ], ALU.mult,
               [("ps", bA), ("rcp", hp)], [("ytmp", hp)])
            amix = cc if gi == 0 else 4
            TT("pool", mixT[hp:hp + 64, amix, q0:q0 + NQ], ytmp[hp:hp + 64, 0:NQ], sga[hp:hp + 64, cc, q0:q0 + NQ], ALU.mult,
               [("ytmp", hp), ("sga", cc, qhf)], [("mix", amix, qhf)])

        att_s1(0)
        for k in range(len(items)):
            if k + 1 < len(items):
                att_s1(k + 1)
            att_s2(k)

        if gi == 1:
            gather(0, 4)
        CK(3 + 10 * (gi * 2 + l))
        def dwconv(src, dst, wcol0, nt, bcol, left, eng="pool", rk=(), wk=()):
            pass

        pf("lru_start")
        ACT(small[:, 8:16], pp[:, PC_LAM:PC_LAM + 8], AF.Exp, ["pp", "small"], ["small"], scale=-1.0)
        ACT(small[:, 8:16], small[:, 8:16], AF.Ln, ["small"], ["small"], scale=1.0, bias=onec)
        TS("dve", small[:, 8:16], small[:, 8:16], -8.0, None, ALU.mult, None, ["small"], ["small"])
        nsp = small[:, 8:16]
        zero_pads()
        for cc in range(NLC):
            def ev_lx(hf, b, cc=cc):
                sl = slice(hf * 512, (hf + 1) * 512)
                TS("dve", xc[:, cc, sl], PB(b)[:, :], pp[:, PC_LCB + cc:PC_LCB + cc + 1], None, ALU.add, None, [("ps", b), "pp"], [("xc", cc)])
                CP("dve", xcb[:, cc, sl], xc[:, cc, sl], [("xc", cc)], [("xcb", cc)])
            proj_conv(LOC["lx"](cc), 4, (lambda j, cc=cc: pp[:, PC_LCW + j * 4 + cc:PC_LCW + j * 4 + cc + 1]), ev_lx)
        pf("after_lx")
        for cc in range(NLC):
            for hf in range(2):
                b = inproj(LOC["lg"](cc), hf)
                ACT(sgl[:, cc, hf * 512:(hf + 1) * 512], PB(b)[:, :], AF.Silu, [("ps", b)], [("sgl", cc, hf)])
        pf("hh0")
        for d in range(2):
            for cc in range(NLC):
                col = d * 4 + cc
                for hf in range(2):
                    sl = slice(hf * 512, (hf + 1) * 512)
                    MM(pst[:, sl], bd[:, (d * 2 + 0) * 4 + cc, :], xcb[:, cc, sl], True, True, ["bd", ("xcb", cc)], [("ps", 4 + hf)])
                for hf in range(2):
                    sl = slice(hf * 512, (hf + 1) * 512)
                    MM(pab[:, sl], bd[:, (d * 2 + 1) * 4 + cc, :], xcb[:, cc, sl], True, True, ["bd", ("xcb", cc)], [("ps", 6 + hf)])
                ACT(rt[:, :], pst[:, :], AF.Sigmoid, [("ps", 4), ("ps", 5), "pp"], ["rt"], bias=pp[:, PC_LBA + col:PC_LBA + col + 1])
                ACT(it[:, :], pab[:, :], AF.Sigmoid, [("ps", 6), ("ps", 7), "pp"], ["it"], bias=pp[:, PC_LBX + col:PC_LBX + col + 1])
                u = (d * NLC + cc) % 2
                av, bv = avs[u], bvs[u]
                ak, bk = [("av", u)], [("bv", u)]
                if u == 1:
                    ak = ak + ["avx"]
                ACT(av[:, :], rt[:, :], AF.Exp, ["rt", "small"], ak, scale=nsp[:, col:col + 1])
                TT("dve", bv[:, :], av[:, :], av[:, :], ALU.mult, ak, bk)
                ACT(bv[:, :], bv[:, :], AF.Ln, bk + ["small"], bk, scale=-1.0, bias=onec)
                ACT(bv[:, :], bv[:, :], AF.Exp, bk, bk, scale=0.5)
                TT("dve", bv[:, :], bv[:, :], it[:, :], ALU.mult, bk + ["it"], bk)
                TT("dve", bv[:, :], bv[:, :], xc[:, cc, :], ALU.mult, bk + [("xc", cc)], bk)
                for s in range(nseq):
                    t0 = s * L
                    if latent:
                        init = h0t[:, (l * 2 + d) * 4 + cc:(l * 2 + d) * 4 + cc + 1]
                    else:
                        init = 0.0
                    if d == 0:
                        o_, a_, b_ = yacc[:, cc, t0:t0 + L], av[:, t0:t0 + L], bv[:, t0:t0 + L]
                    else:
                        o_, a_, b_ = hsf[:, t0:t0 + L][:, ::-1], av[:, t0:t0 + L][:, ::-1], bv[:, t0:t0 + L][:, ::-1]
                    S.op("dve", (lambda o_, a_, b_, init: lambda e: e.tensor_tensor_scan(out=o_, data0=a_, data1=b_, initial=init, op0=ALU.mult, op1=ALU.add))(o_, a_, b_, init),
                         ak + bk + ["h0t"], [("yacc", cc)] if d == 0 else ["hsf"])
                if not latent:
                    for s in range(nseq):
                        t0 = s * L
                        if d == 0:
                            CP("pool", HL[:, cc, s * 2:s * 2 + 1], yacc[:, cc, t0 + L - 1:t0 + L], [("yacc", cc)], ["HL"])
                        else:
                            CP("pool", HL[:, cc, s * 2 + 1:s * 2 + 2], hsf[:, t0:t0 + 1], ["hsf"], ["HL"])
                if d == 1:
                    TT("dve", yacc[:, cc, :], yacc[:, cc, :], hsf[:, :], ALU.add, [("yacc", cc), "hsf"], [("yacc", cc)])
                    for hf in range(2):
                        sl = slice(hf * 512, (hf + 1) * 512)
                        lmix = (8 + cc) if gi == 0 else 5
                        TT("pool", mixT[:, lmix, sl], yacc[:, cc, sl], sgl[:, cc, sl], ALU.mult, [("yacc", cc), ("sgl", cc, hf)], [("mix", lmix, hf)])
        if not latent:
            for cc in range(NLC):
                DMA(nl_d[l].rearrange("q (c p) -> p c q", p=128)[:, cc, :], HL[:, cc, :], ["HL"], ())

        if gi == 1:
            gather(2, 5)
        CK(4 + 10 * (gi * 2 + l))
        fwdv = (lambda jt, c0: fwd256[:, jt, c0:c0 + 128]) if L == 256 else (lambda jt, c0: dft.rearrange("p (t r) -> p t r", t=8)[:, jt, c0:c0 + 128])
        invv = (lambda rc, t0, n: inv256[:, rc, t0:t0 + n]) if L == 256 else (lambda rc, t0, n: dft.rearrange("p (t r) -> p t r", t=16)[:, rc, t0:t0 + n])
        dkey = "ropec" if L == 256 else "dft"
        ikey = "ropes" if L == 256 else "dft"

        wb0v = wbuf[0][:, :, :].rearrange("p k c -> p (k c)").rearrange("p (k c) -> p k c", k=4)

        def load_wout():
            wov = wout_d[l].rearrange("(k p) c -> p k c", p=128)
            if gi == 0:
                for ch_ in range(2):
                    DMA(wo_full[:, :, ch_ * 512:(ch_ + 1) * 512], wov[:, :, ch_ * 512:(ch_ + 1) * 512], (), [("wof", ch_)], q="pool")
            else:
                for h4 in range(2):
                    DMA(hT[:, 4 * h4:4 * h4 + 4, :], wov[:, 4 * h4:4 * h4 + 4, :], (),
                        [("hT", k_, b_) for k_ in range(4 * h4, 4 * h4 + 4) for b_ in range(2)], q="pool")
                DMA(wb0v, wov[:, 8:12, :], (), [("win", 0)], q="pool")

        def wo_rhs(kc, ch):
            if gi == 0:
                return wo_full[:, kc, ch * 512:(ch + 1) * 512], ("wof", ch)
            if kc < 8:
                return hT[:, kc, ch * 512:(ch + 1) * 512], ("hT", kc, ch)
            return wb0v[:, kc - 8, ch * 512:(ch + 1) * 512], ("win", 0)

        if L == 256:
            load_wout()
        hmix = (lambda hh_, c2_: 4 + hh_ * 2 + c2_) if gi == 0 else (lambda hh_, c2_: 6)
        if L != 256 and HH == 1:
            DMA(dft.rearrange("p (t r) -> p t r", t=8), fwd_d[L].rearrange("(t p) r -> p t r", p=128), (), ["dft"], q="pool")
        for hh in range(HH):
            pf("hh%d" % hh)
            for c2 in range(NC2):
                chn = 0 * 4 + hh * 2 + c2
                def ev_x0(hf, b, c2=c2, chn=chn):
                    sl = slice(hf * 512, (hf + 1) * 512)
                    TS("dve", x0c[:, c2, sl], PB(b)[:, :], pp[:, PC_HSB + chn:PC_HSB + chn + 1], None, ALU.add, None, [("ps", b), "pp"], [("x0c", c2)])
                proj_conv(LOC["x0"](hh, c2), 3, (lambda j, chn=chn: pp[:, PC_HSW + j * 12 + chn:PC_HSW + j * 12 + chn + 1]), ev_x0)
            ykeys = [("Y", s_) for s_ in range(nseq)]
            for c2 in range(NC2):
                chn = 1 * 4 + hh * 2 + c2
                def ev_x1(hf, b, chn=chn):
                    sl = slice(hf * 512, (hf + 1) * 512)
                    TS("dve", cvt[:, sl], PB(b)[:, :], pp[:, PC_HSB + chn:PC_HSB + chn + 1], None, ALU.add, None, [("ps", b), "pp"], ykeys)
                proj_conv(LOC["x1"](hh, c2), 3, (lambda j, chn=chn: pp[:, PC_HSW + j * 12 + chn:PC_HSW + j * 12 + chn + 1]), ev_x1)
                chn = 8 + hh * 2 + c2
                def ev_v(hf, b, c2=c2, chn=chn):
                    sl = slice(hf * 512, (hf + 1) * 512)
                    STT("dve", zT[:, c2, sl], PB(b)[:, :], pp[:, PC_HSB + chn:PC_HSB + chn + 1], cvt[:, sl], ALU.add, ALU.mult,
                        [("ps", b), "pp"] + ykeys, [("zT", c2)])
                proj_conv(LOC["hv"](hh, c2), 3, (lambda j, chn=chn: pp[:, PC_HSW + j * 12 + chn:PC_HSW + j * 12 + chn + 1]), ev_v)
            pf("after_hx%d" % hh)
            for c2 in range(NC2):
                for hf in range(2):
                    b = inproj(LOC["hg"](hh, c2), hf)
                    ACT(sgh[:, c2, hf * 512:(hf + 1) * 512], PB(b)[:, :], AF.Silu, [("ps", b)], [("sgh", c2, hf)])
            if gi == 1 and hh == HH - 1:
                load_wout()
            for j in range(8):
                b = bank("tr")
                pbv = PB(b)[:, :].bitcast(BF16)
                for c2 in range(NC2):
                    TR(pbv[:, c2 * 128:(c2 + 1) * 128], zT[:, c2, j * 128:(j + 1) * 128], [("zT", c2)], [("ps", b)])
                CP("act", ztok[:, j, 0:CW], pbv[:, 0:CW], [("ps", b)], [("ztok", j)])
            DMA(w3s[:, 0:2 * CW], (w3_d if gi == 0 else w3ss_d)[l][:, hh * 512:hh * 512 + 2 * CW], (), ["w3s"])
            DMA(decbc[:, 0:2 * CW], (dec_d if gi == 0 else decs_d)[l][hh * 512:hh * 512 + 2 * CW].partition_broadcast(128), (), ["decbc"])
            bN = 7
            W2 = 2 * CW
            sc3b = sc3[:, 0:256].bitcast(BF16)

            def taps(lt_):
                b_ = bank("mm")
                MM(PB(b_)[:, 0:W2], hdn1[0:64, lt_ * 128:(lt_ + 1) * 128], w3s[0:64, 0:W2], True, True,
                   ["hdn1", "w3s"], [("ps", b_)])
                return b_

            bnext = taps(0)
            for lt in range(nf):
                b = bnext
                if lt + 1 < nf:
                    bnext = taps(lt + 1)
                ACT(sc1[:, 0:W2], decbc[:, 0:W2], AF.Exp, ["decbc", "ntcol"], ["sc1"], scale=ntcol[:, lt:lt + 1])
                TT("dve", sc2[:, 0:W2], PB(b)[:, 0:W2], sc1[:, 0:W2], ALU.mult, [("ps", b), "sc1"], ["sc2"])
                if lt == 0:
                    MEMSET("dve", sc2[0:1, CW:W2], 0.0, ["sc2"])
                STT("pool", sc3b[:, 0:W2], sc2[:, 0:W2], -1.0, sc2[:, 0:W2], ALU.mult, ALU.max, ["sc2"], ["sc3"])
                MM(PB(bN)[:, 0:W2], onesb[:], sc3b[:, 0:W2], lt == 0, lt == nf - 1, ["onesb", "sc3"], [("ps", bN)])
                TT("pool", filt[:, lt, 0:CW], sc2[:, 0:CW], sc2[:, CW:W2], ALU.add, ["sc2"], [("filt", lt)])
                TT("pool", filt[:, lt, 256:256 + CW], sc2[:, 0:CW], sc2[:, CW:W2], ALU.subtract, ["sc2"], [("filt", lt)])
            CP("act", sc1[:, 0:2 * CW], PB(bN)[:, 0:2 * CW], [("ps", bN)], ["sc1"])
            TT("dve", rnb[:, 0:CW], sc1[:, 0:CW], sc1[:, CW:2 * CW], ALU.add, ["sc1"], ["rnb"])
            S.op("dve", (lambda cw: lambda e: e.reciprocal(out=rnb[:, 0:cw], in_=rnb[:, 0:cw]))(CW), ["rnb"], ["rnb"])
            if L != 256 and HH != 1:
                DMA(dft.rearrange("p (t r) -> p t r", t=8), fwd_d[L].rearrange("(t p) r -> p t r", p=128), (), ["dft"])
            Yv = Ypk[:, 0:nseq * 2 * nf * CW].rearrange("p (s r c) -> p s r c", s=nseq, r=2 * nf)
            for i in range(nf):
                b = bank("mm")
                for jt in range(nf):
                    MM(PB(b)[:, 0:CW], fwdv(jt, i * 128), filt[:, jt, 0:CW], jt == 0, jt == nf - 1, [dkey, ("filt", jt)], [("ps", b)])
                for jt in range(nf):
                    MM(PB(b)[:, 256:256 + CW], fwdv(jt, (nf + i) * 128), filt[:, jt, 256:256 + CW], jt == 0, jt == nf - 1, [dkey, ("filt", jt)], [("ps", b)])
                TT("dve", Hs[:, 0, 0:CW], PB(b)[:, 0:CW], rnb[:, 0:CW], ALU.mult, [("ps", b), "rnb"], ["Hs"])
                TT("dve", Hs[:, 1, 0:CW], PB(b)[:, 256:256 + CW], rnb[:, 0:CW], ALU.mult, [("ps", b), "rnb"], ["Hs"])
                if i == 0:
                    b2 = bank("mm")
                    for jt in range(nf):
                        MM(PB(b2)[:, 0:CW], fwdv(jt, nf * 128), filt[:, jt, 0:CW], jt == 0, jt == nf - 1, [dkey, ("filt", jt)], [("ps", b2)])
                    TT("dve", Hs[0:1, 1, 0:CW], PB(b2)[0:1, 0:CW], rnb[0:1, 0:CW], ALU.mult, [("ps", b2), "rnb", "Hs"], ["Hs"])
                for s in range(nseq):
                    bz = bank("st")
                    for jt in range(nf):
                        MM(PB(bz)[:, 0:CW], fwdv(jt, i * 128), ztok[:, s * ntl + jt, 0:CW], jt == 0, jt == nf - 1, [dkey, ("ztok", s * ntl + jt)], [("ps", bz)])
                    for jt in range(nf):
                        MM(PB(bz)[:, 256:256 + CW], fwdv(jt, (nf + i) * 128), ztok[:, s * ntl + jt, 0:CW], jt == 0, jt == nf - 1, [dkey, ("ztok", s * ntl + jt)], [("ps", bz)])
                    zre, zim = PB(bz)[:, 0:CW], PB(bz)[:, 256:256 + CW]
                    zk = [("ps", bz)]
                    TT("dve", hyt1[:, 0:CW], zre, Hs[:, 0, 0:CW], ALU.mult, zk + ["Hs"], ["hyt1"])
                    TT("dve", hyt2[:, 0:CW], zim, Hs[:, 1, 0:CW], ALU.mult, zk + ["Hs"], ["hyt2"])
                    TT("pool", Yv[:, s, i, :], hyt1[:, 0:CW], hyt2[:, 0:CW], ALU.subtract, ["hyt1", "hyt2"], [("Y", s)])
                    TT("dve", hyt3[:, 0:CW], zre, Hs[:, 1, 0:CW], ALU.mult, zk + ["Hs"], ["hyt3"])
                    TT("dve", sc3[:, 0:CW], zim, Hs[:, 0, 0:CW], ALU.mult, zk + ["Hs"], ["sc3"])
                    TT("pool", Yv[:, s, nf + i, :], hyt3[:, 0:CW], sc3[:, 0:CW], ALU.add, ["hyt3", "sc3"], [("Y", s)])
                    if i == 0:
                        TT("dve", Yv[0:1, s, 0, :], PB(bz)[0:1, 0:CW], Hs[0:1, 0, 0:CW], ALU.mult, zk + ["Hs", ("Y", s)], [("Y", s)])
                        TT("dve", Yv[0:1, s, nf, :], PB(bz)[0:1, 256:256 + CW], Hs[0:1, 1, 0:CW], ALU.mult, zk + ["Hs", ("Y", s)], [("Y", s)])
            if L != 256:
                DMA(dft.rearrange("p (t r) -> p t r", t=16), inv_d[L].rearrange("(t p) r -> p t r", p=128), (), ["dft"])
            for s in range(nseq):
                for c2 in range(NC2):
                    for tb in range(nqb):
                        tt0 = s * L + tb * NQ
                        hf = tt0 // 512
                        b = bank("mm")
                        for rc in range(2 * nf):
                            MM(PB(b)[:, 0:NQ], Yv[:, s, rc, c2 * 128:(c2 + 1) * 128], invv(rc, tb * NQ, NQ), rc == 0, rc == 2 * nf - 1,
                               [("Y", s), ikey], [("ps", b)])
                        hbcol = PC_HYB + hh * 2 + c2
                        STT("dve", sc1[:, 0:NQ], zT[:, c2, tt0:tt0 + NQ], pp[:, hbcol:hbcol + 1], PB(b)[:, 0:NQ], ALU.mult, ALU.add,
                            [("zT", c2), "pp", ("ps", b)], ["sc1"])
                        TT("pool", sc2[:, 0:NQ], sc1[:, 0:NQ], x0c[:, c2, tt0:tt0 + NQ], ALU.mult, ["sc1", ("x0c", c2)], ["sc2"])
                        TT("pool", mixT[:, hmix(hh, c2), tt0:tt0 + NQ], sc2[:, 0:NQ], sgh[:, c2, tt0:tt0 + NQ], ALU.mult,
                           ["sc2", ("sgh", c2, hf)], [("mix", hmix(hh, c2), hf)])

        CK(5 + 10 * (gi * 2 + l))
        if gi == 1:
            gather(1, 6)
        kc_sets = [list(range(12))] if gi == 0 else [[0, 1, 2, 3, 8, 9, 10, 11], [4, 5, 6, 7]]
        for kcs in kc_sets:
            for j in range(8):
                for ch in range(2):
                    b = bank("mm")
                    for ki, kc in enumerate(kcs):
                        rhs_, rk_ = wo_rhs(kc, ch)
                        MM(PB(b)[:, :], mixT[:, kc, j * 128:(j + 1) * 128], rhs_, ki == 0, ki == len(kcs) - 1,
                           [("mix", kc, j // 4), rk_], [("ps", b)])
                    xs_ = x_t[:, j, ch * 512:(ch + 1) * 512]
                    TT("dve", sc1[:, :], PB(b)[:, :], gate_bc[:, ch * 512:(ch + 1) * 512], ALU.mult, [("ps", b), ("gate", ch)], ["sc1"])
                    TT("pool", xs_, xs_, sc1[:, :], ALU.add, [("x", j), "sc1"], [("x", j)])

    def conv3(src, skey, dst, dkey_, chn, nseq, L):
        dkl = list(dkey_) if isinstance(dkey_, list) else [dkey_]
        w = lambda j: pp[:, PC_HSW + j * 12 + chn:PC_HSW + j * 12 + chn + 1]
        src3 = src.rearrange("p (s n) -> p s n", s=nseq)
        dst3 = dst.rearrange("p (s n) -> p s n", s=nseq)
        TS("pool", dst, src, w(1), pp[:, PC_HSB + chn:PC_HSB + chn + 1], ALU.mult, ALU.add, [skey, "pp"], dkl)
        STT("pool", dst3[:, :, 1:L], src3[:, :, 0:L - 1], w(0), dst3[:, :, 1:L], ALU.mult, ALU.add, [skey, "pp"] + dkl, dkl)
        STT("pool", dst3[:, :, 0:L - 1], src3[:, :, 1:L], w(2), dst3[:, :, 0:L - 1], ALU.mult, ALU.add, [skey, "pp"] + dkl, dkl)

    ropec = sb("ropec", [128, 1024], BF16)
    ropes = sb("ropes", [128, 1024], BF16)
    fwd256 = ropec[:, :].rearrange("p (t r) -> p t r", t=2)
    inv256 = ropes[:, :].rearrange("p (t r) -> p t r", t=4)
    DMA(fwd256, fwd_d[256].rearrange("(t p) r -> p t r", p=128), (), ["ropec"])
    DMA(inv256, inv_d[256].rearrange("(t p) r -> p t r", p=128), (), ["ropes"])
    wo_full = ar_a[:, 0:6144].bitcast(BF16).rearrange("p (k c) -> p k c", k=12)
    S.alias["wof"] = ("ar_a", "op")

    groups = [(0, 4, 256), (1, 1, 1024)]
    try:
      startup()
      CK(1)
      for gi, nseq, L in groups:
          if gi == 1:
              DMA(ropec[:], cos_d, (), ["ropec"])
              DMA(ropes[:], sin_d, (), ["ropes"])
          if gi == 0:
              for j in range(8):
                  DMA(x_t[:, j, :], xg_d[gi][j * 128:(j + 1) * 128, :], (), [("x", j)])
          for l in range(2):
              layer_pass(gi, l, nseq, L)
          DMA(gate_bc[:], fg_d.partition_broadcast(128), (), [("gate", 0), ("gate", 1)])
          junk = ar_c[:, 0:512].bitcast(BF16)
          for j in range(8):
              ACT(junk, x_t[:, j, :], AF.Square, [("x", j)], ["modtmp", ("ssqf", j)], accum=ssq[:, j:j + 1])
          ACT(rstd[:, 0:8], ssq[:, 0:8], AF.Ln, [("ssqf", j_) for j_ in range(8)] + ["small"], [("rstdf", 0)], scale=1.0 / D, bias=epsc)
          ACT(rstd[:, 0:8], rstd[:, 0:8], AF.Exp, [("rstdf", 0)], [("rstdf", 0)], scale=-0.5)
          for j in range(8):
              yst = ust[:, j % 2, :]
              STT("dve", yst, x_t[:, j, :], rstd[:, j:j + 1], gate_bc[:], ALU.mult, ALU.mult, [("x", j), ("rstdf", 0), ("gate", 0), ("gate", 1)], [("ust", j % 2)])
              if gi == 0:
                  DMA(x_t[:, j, :], xg_d[1][j * 128:(j + 1) * 128, :], (), [("x", j)])
              DMA(yg_d[gi][j * 128:(j + 1) * 128, :], yst, [("ust", j % 2)], ())
    except _StopBuild:
        pass
    with nc.allow_non_contiguous_dma(reason="tiny strided state/param transfers"):
        info = S.finalize()
    info["sbuf_left"] = nc.sbuf_bytes_remaining
    return nc, info


def _consts():
    c = {}
    c["ident"] = np.eye(128, dtype=np.float32).astype(ml_dtypes.bfloat16)
    p = np.zeros((128, 128), np.float32)
    for m in range(128):
        p[m ^ 1, m] = 1.0
    c["psw"] = p.astype(ml_dtypes.bfloat16)
    bo = np.zeros((128, 128), np.float32)
    bo[0:64, 0:64] = 1.0
    bo[64:128, 64:128] = 1.0
    c["bones"] = bo.astype(ml_dtypes.bfloat16)
    for L in (256, 1024):
        t = np.linspace(0.0, 1.0, L, dtype=np.float32)
        w = (2.0 * np.pi * np.arange(L, dtype=np.float32) / L).astype(np.float32)
        f = np.linspace(1e-4, 15, 16, dtype=np.float32)
        fw = (f[None, :] * w[:, None]).astype(np.float32)
        feats = np.concatenate([t[:, None], np.cos(fw), -np.sin(fw)], axis=-1).astype(np.float32)
        c["ft%d" % L] = np.ascontiguousarray(feats.T)
        c["tc%d" % L] = np.ascontiguousarray(t.reshape(L // 128, 128).T)
        th = np.pi / L
        j = np.arange(L, dtype=np.float64)[:, None]
        fwd = np.zeros((L, 2 * L), np.float64)
        fr = np.arange(L, dtype=np.float64)[None, :]
        fwd[:, 0:L] = np.cos(th * j * fr)
        fwd[:, L] = np.cos(np.pi * j[:, 0])
        fi = np.arange(1, L, dtype=np.float64)[None, :]
        fwd[:, L + 1:2 * L] = -np.sin(th * j * fi)
        c["fwd%d" % L] = fwd.astype(np.float32).astype(ml_dtypes.bfloat16)
        tt = np.arange(L, dtype=np.float64)[None, :]
        inv = np.zeros((2 * L, L), np.float64)
        fcol = np.arange(L, dtype=np.float64)[:, None]
        inv[0:L, :] = np.cos(th * fcol * tt) / L
        inv[0, :] = 1.0 / (2 * L)
        inv[L, :] = np.cos(np.pi * tt[0]) / (2 * L)
        fic = np.arange(1, L, dtype=np.float64)[:, None]
        inv[L + 1:2 * L, :] = -np.sin(th * fic * tt) / L
        c["inv%d" % L] = inv.astype(np.float32).astype(ml_dtypes.bfloat16)
    tok = np.arange(1024)
    row = (tok // 64).astype(np.float32)
    col = (tok % 64).astype(np.float32)
    nfq = 16
    invf = (10000.0 ** (-np.arange(nfq, dtype=np.float32) / nfq)).astype(np.float32)
    ang = np.concatenate([row[:, None] * invf, col[:, None] * invf], axis=-1).astype(np.float32)
    cosv = np.cos(ang).astype(np.float32)
    sinv = np.sin(ang).astype(np.float32)
    rc = np.zeros((128, 1024), np.float32)
    rs = np.zeros((128, 1024), np.float32)
    for p_ in range(128):
        d = p_ % 64
        i = d // 2
        rc[p_] = cosv[:, i]
        rs[p_] = (-sinv[:, i]) if d % 2 == 0 else sinv[:, i]
    c["ropec"] = rc.astype(ml_dtypes.bfloat16)
    c["ropes"] = rs.astype(ml_dtypes.bfloat16)
    return c


_CACHE = {}


def kernel(x_prompt, x_sample, cache_k, cache_v, state_lru, c, c_ctx, norm_g, w_ada, b_ada, w_in,
           q_norm_g, k_norm_g, hy_short_w, hy_short_b, hy_filt_w1, hy_filt_b1, hy_filt_w2, hy_filt_b2,
           hy_filt_w3, hy_filt_freq, hy_filt_decay, hy_bias, lru_conv_w, lru_conv_b, lru_wa, lru_ba,
           lru_wx, lru_bx, lru_lambda, w_out, final_g):
    f = lambda a: np.ascontiguousarray(np.asarray(a, dtype=np.float32))
    x_prompt, x_sample, cache_k, cache_v, state_lru, c, c_ctx = map(f, (x_prompt, x_sample, cache_k, cache_v, state_lru, c, c_ctx))
    w_in = f(w_in)
    Q0, K0, V0, GA0, HX0, HX1, HV, GH, LX, GL = 0, 512, 640, 768, 1280, 1792, 2304, 2816, 3328, 3840
    cols = []
    cols += list(range(Q0, Q0 + 512))
    cols += list(range(K0, K0 + 64)) * 2 + list(range(K0 + 64, K0 + 128)) * 2
    cols += list(range(V0, V0 + 64)) * 2 + list(range(V0 + 64, V0 + 128)) * 2
    cols += list(range(GA0, GA0 + 512))
    cols += list(range(LX, LX + 512))
    cols += list(range(GL, GL + 512))
    for hh in range(2):
        cols += list(range(HX0 + hh * 256, HX0 + hh * 256 + 256)) + list(range(HX1 + hh * 256, HX1 + hh * 256 + 256))
        cols += list(range(HV + hh * 256, HV + hh * 256 + 256)) + list(range(GH + hh * 256, GH + hh * 256 + 256))
    cols = np.array(cols)
    assert cols.size == NG * 512
    w_in_r = np.ascontiguousarray(w_in[:, :, cols])
    ppack = np.zeros((2, 128, PC_N), np.float32)
    lru_bd = np.zeros((2, 128, 16, 128), np.float32)
    w3r = np.zeros((2, 64, 1024), np.float32)
    decr = np.zeros((2, 1024), np.float32)
    g = lambda a: np.asarray(a, dtype=np.float32)
    for l in range(2):
        ppack[l, :, PC_NG:PC_NG + 8] = g(norm_g)[l].reshape(8, 128).T
        ppack[l, :, PC_HSW:PC_HSW + 36] = g(hy_short_w)[l].reshape(3, 12, 128).transpose(2, 0, 1).reshape(128, 36)
        ppack[l, :, PC_HSB:PC_HSB + 12] = g(hy_short_b)[l].reshape(12, 128).T
        ppack[l, :, PC_HYB:PC_HYB + 4] = g(hy_bias)[l].reshape(4, 128).T
        ppack[l, :, PC_LCW:PC_LCW + 16] = g(lru_conv_w)[l].reshape(4, 4, 128).transpose(2, 0, 1).reshape(128, 16)
        ppack[l, :, PC_LCB:PC_LCB + 4] = g(lru_conv_b)[l].reshape(4, 128).T
        ppack[l, :, PC_LBA:PC_LBA + 8] = g(lru_ba)[l].reshape(2, 4, 128).transpose(2, 0, 1).reshape(128, 8)
        ppack[l, :, PC_LBX:PC_LBX + 8] = g(lru_bx)[l].reshape(2, 4, 128).transpose(2, 0, 1).reshape(128, 8)
        ppack[l, :, PC_LAM:PC_LAM + 8] = g(lru_lambda)[l].reshape(2, 4, 128).transpose(2, 0, 1).reshape(128, 8)
        ppack[l, :, PC_QG] = np.tile(g(q_norm_g)[l], 2)
        ppack[l, :, PC_KG] = np.tile(g(k_norm_g)[l], 2)
        ppack[l, 0:64, PC_B1] = g(hy_filt_b1)[l]
        ppack[l, 0:64, PC_B2] = g(hy_filt_b2)[l]
        ppack[l, 0:64, PC_F0] = g(hy_filt_freq)[l, 0]
        ppack[l, 0:64, PC_F1] = g(hy_filt_freq)[l, 1]
        for d in range(2):
            for gt, wsrc in enumerate((g(lru_wa), g(lru_wx))):
                for cc in range(4):
                    n = (d * 2 + gt) * 4 + cc
                    lru_bd[l, 0:64, n, 0:64] = wsrc[l, d, 2 * cc]
                    lru_bd[l, 64:128, n, 64:128] = wsrc[l, d, 2 * cc + 1]
        w3 = g(hy_filt_w3)[l]
        dc = g(hy_filt_decay)[l]
        for hh in range(2):
            w3r[l, :, hh * 512:hh * 512 + 256] = w3[:, hh * 256:hh * 256 + 256]
            w3r[l, :, hh * 512 + 256:hh * 512 + 512] = w3[:, 512 + hh * 256:512 + hh * 256 + 256]
            decr[l, hh * 512:hh * 512 + 256] = dc[hh * 256:hh * 256 + 256]
            decr[l, hh * 512 + 256:hh * 512 + 512] = dc[512 + hh * 256:512 + hh * 256 + 256]
    perm = []
    for r_ in range(4):
        perm += list(range(r_ * 128, r_ * 128 + 128)) + list(range(512 + r_ * 128, 512 + r_ * 128 + 128)) + list(range(1024 + r_ * 128, 1024 + r_ * 128 + 128))
    w_out_s = np.ascontiguousarray(f(w_out)[:, np.array(perm), :])
    share = []
    for r_ in range(4):
        gk = r_ // 2
        sc_ = []
        sc_ += list(range(Q0 + r_ * 128, Q0 + r_ * 128 + 128))
        sc_ += list(range(K0 + gk * 64, K0 + gk * 64 + 64)) * 2
        sc_ += list(range(V0 + gk * 64, V0 + gk * 64 + 64)) * 2
        sc_ += list(range(GA0 + r_ * 128, GA0 + r_ * 128 + 128))
        sc_ += list(range(LX + r_ * 128, LX + r_ * 128 + 128))
        sc_ += list(range(GL + r_ * 128, GL + r_ * 128 + 128))
        sc_ += list(range(HX0 + r_ * 128, HX0 + r_ * 128 + 128))
        sc_ += list(range(HX1 + r_ * 128, HX1 + r_ * 128 + 128))
        sc_ += list(range(HV + r_ * 128, HV + r_ * 128 + 128))
        sc_ += list(range(GH + r_ * 128, GH + r_ * 128 + 128))
        wsh = np.zeros((2, D, 1536), np.float32)
        wsh[:, :, 0:1280] = w_in[:, :, np.array(sc_)]
        pps = ppack.copy()
        bds = np.zeros((2, 128, 16, 128), np.float32)
        w3ss = np.zeros((2, 64, 256), np.float32)
        decs = np.zeros((2, 256), np.float32)
        for l in range(2):
            for j_ in range(3):
                pps[l, :, PC_HSW + j_ * 12 + 0] = g(hy_short_w)[l, j_, r_ * 128:(r_ + 1) * 128]
                pps[l, :, PC_HSW + j_ * 12 + 4] = g(hy_short_w)[l, j_, 512 + r_ * 128:512 + (r_ + 1) * 128]
                pps[l, :, PC_HSW + j_ * 12 + 8] = g(hy_short_w)[l, j_, 1024 + r_ * 128:1024 + (r_ + 1) * 128]
            pps[l, :, PC_HSB + 0] = g(hy_short_b)[l, r_ * 128:(r_ + 1) * 128]
            pps[l, :, PC_HSB + 4] = g(hy_short_b)[l, 512 + r_ * 128:512 + (r_ + 1) * 128]
            pps[l, :, PC_HSB + 8] = g(hy_short_b)[l, 1024 + r_ * 128:1024 + (r_ + 1) * 128]
            pps[l, :, PC_HYB + 0] = g(hy_bias)[l, r_ * 128:(r_ + 1) * 128]
            for j_ in range(4):
                pps[l, :, PC_LCW + j_ * 4 + 0] = g(lru_conv_w)[l, j_, r_ * 128:(r_ + 1) * 128]
            pps[l, :, PC_LCB + 0] = g(lru_conv_b)[l, r_ * 128:(r_ + 1) * 128]
            for d in range(2):
                pps[l, :, PC_LBA + d * 4 + 0] = g(lru_ba)[l, d, r_ * 128:(r_ + 1) * 128]
                pps[l, :, PC_LBX + d * 4 + 0] = g(lru_bx)[l, d, r_ * 128:(r_ + 1) * 128]
                pps[l, :, PC_LAM + d * 4 + 0] = g(lru_lambda)[l, d, r_ * 128:(r_ + 1) * 128]
                for gt, wsrc in enumerate((g(lru_wa), g(lru_wx))):
                    n = (d * 2 + gt) * 4 + 0
                    bds[l, 0:64, n, 0:64] = wsrc[l, d, 2 * r_]
                    bds[l, 64:128, n, 64:128] = wsrc[l, d, 2 * r_ + 1]
            w3 = g(hy_filt_w3)[l]
            dc = g(hy_filt_decay)[l]
            w3ss[l, :, 0:128] = w3[:, r_ * 128:(r_ + 1) * 128]
            w3ss[l, :, 128:256] = w3[:, 512 + r_ * 128:512 + (r_ + 1) * 128]
            decs[l, 0:128] = dc[r_ * 128:(r_ + 1) * 128]
            decs[l, 128:256] = dc[512 + r_ * 128:512 + (r_ + 1) * 128]
        share.append(dict(w_in_s=wsh, pps=pps, bds=bds.reshape(2, 128, 2048), w3ss=w3ss, decs=decs))
    consts = _consts()
    shared = dict(w_out_s=w_out_s, w_ada=f(w_ada), b_ada=f(b_ada), w_in_r=w_in_r, w_out=f(w_out), ppack=ppack,
                  lru_bd=lru_bd.reshape(2, 128, 2048), w1=f(hy_filt_w1), w2=f(hy_filt_w2), w3r=w3r, decr=decr,
                  final_g=f(final_g), **consts)
    in_maps = []
    for ci in range(NCORES):
        b = ci % 2
        m = dict(shared)
        m["xp"] = x_prompt[4 * ci:4 * ci + 4].reshape(1024, D)
        m["xs"] = x_sample[b]
        r_ = ci // 2
        gk = r_ // 2
        m.update(share[r_])
        m["ckT"] = np.ascontiguousarray(cache_k[b][:, :, gk, :].transpose(0, 2, 1))
        m["cv"] = np.ascontiguousarray(cache_v[b][:, :, gk, :])
        stt = np.zeros((128, 16), np.float32)
        for l_ in range(2):
            for d_ in range(2):
                stt[:, (l_ * 2 + d_) * 4 + 0] = state_lru[b, l_, d_, r_ * 128:(r_ + 1) * 128]
        m["st"] = stt
        cp = np.stack([c_ctx.reshape(8, 128), c[b].reshape(8, 128)], axis=-1)
        m["cpack"] = np.ascontiguousarray(cp.transpose(1, 0, 2).reshape(128, 16))
        in_maps.append(m)
    if "nc" not in _CACHE:
        _CACHE["nc"], _CACHE["info"] = build_program()
    nc = _CACHE["nc"]
    res = run_bass_kernel_spmd(nc, in_maps, core_ids=list(range(NCORES)))
    R = res.results
    y_prompt = np.concatenate([R[i]["yp"].reshape(4, 256, D) for i in range(NCORES)], axis=0)
    y_sample = np.stack([R[0]["ys"], R[1]["ys"]], axis=0)
    nk = np.concatenate([R[i]["nk"].reshape(2, 4, 256, 2, 64).transpose(1, 0, 2, 3, 4) for i in range(NCORES)], axis=0)
    nv = np.concatenate([R[i]["nv"].reshape(2, 4, 256, 2, 64).transpose(1, 0, 2, 3, 4) for i in range(NCORES)], axis=0)
    nl = np.concatenate([R[i]["nl"].reshape(2, 4, 2, 512).transpose(1, 0, 2, 3) for i in range(NCORES)], axis=0)
    return (y_prompt.astype(np.float32), y_sample.astype(np.float32), np.ascontiguousarray(nk, dtype=np.float32),
            np.ascontiguousarray(nv, dtype=np.float32), np.ascontiguousarray(nl, dtype=np.float32))
```

```python
import math
import os
import numpy as np
import ml_dtypes
import concourse.bass as bass
import concourse.mybir as mybir
from concourse.bass_utils import run_bass_kernel_spmd

F32 = mybir.dt.float32
BF16 = mybir.dt.bfloat16
I32 = mybir.dt.int32
ALU = mybir.AluOpType
AF = mybir.ActivationFunctionType

COMPUTE = ("pe", "act", "dve", "pool")
NDMASEM = 12
NCORES = 8
D = 1024
EPS = 1e-6
NG = 9


STOP = float(os.environ.get("KSTOP", "99"))


class _StopBuild(Exception):
    pass


def CK(n):
    if STOP <= n:
        raise _StopBuild()


class Sched:
    def __init__(self, nc):
        self.nc = nc
        self.ops = []
        self.last_writer = {}
        self.readers = {}
        self.alias = {}
        self.regst = {}

    def _summ(self, cur):
        last = {}
        out = []
        for i in cur:
            o = self.ops[i]
            if o["dma"]:
                out.append(i)
            else:
                last[o["eng"]] = i
        return out + list(last.values())

    def op(self, eng, emit, reads=(), writes=(), dma=False, cc=False):
        idx = len(self.ops)
        deps = {}
        regs = set()
        for k in list(reads) + list(writes):
            n = k[0] if isinstance(k, tuple) else k
            if n in self.alias:
                regs.add(self.alias[n])
        for region, owner in regs:
            st = self.regst.setdefault(region, dict(owner=None, cur=[], prev=[]))
            if st["owner"] != owner:
                st["prev"] = self._summ(st["cur"])
                st["cur"] = []
                st["owner"] = owner
            for i in st["prev"]:
                deps[i] = True
            st["cur"].append(idx)
        for k in reads:
            w = self.last_writer.get(k)
            if w is not None:
                deps[w] = True
        for k in writes:
            w = self.last_writer.get(k)
            if w is not None:
                deps.setdefault(w, False)
            seen = set()
            for r in reversed(self.readers.get(k, ())):
                if r == idx:
                    continue
                ro = self.ops[r]
                if not ro["dma"]:
                    if ro["eng"] in seen:
                        continue
                    seen.add(ro["eng"])
                deps.setdefault(r, False)
        for k in reads:
            self.readers.setdefault(k, []).append(idx)
        for k in writes:
            self.last_writer[k] = idx
            self.readers[k] = []
        self.ops.append(dict(eng=eng, emit=emit, deps=deps, dma=dma or cc, idx=idx, cc=cc))
        return idx

    def finalize(self):
        nc = self.nc
        ops = self.ops

        def skip_edge(o, do, raw):
            if do["dma"] or o["dma"]:
                return False
            if do["eng"] == o["eng"]:
                return o["eng"] == "pe" or not raw
            return False

        needed = set()
        for o in ops:
            for d, raw in o["deps"].items():
                if not skip_edge(o, ops[d], raw):
                    needed.add(d)
        sems = {e: nc.alloc_semaphore("sem_" + e) for e in COMPUTE}
        dsems = {q: [nc.alloc_semaphore("dsem_%s_%d" % (q, i)) for i in range(NDMASEM)] for q in ("sp", "pool")}
        tick = {e: 0 for e in COMPUTE}
        dcount = {q: 0 for q in dsems}
        duse = {q: [0] * NDMASEM for q in dsems}
        ccsem = nc.alloc_semaphore("ccsem")
        ccn = 0
        for o in ops:
            if o["cc"]:
                ccn += 1
                o["sem"] = ccsem
                o["val"] = ccn
                o["prev"] = 0
            elif o["dma"]:
                q = o["eng"]
                s = dcount[q] % NDMASEM
                dcount[q] += 1
                duse[q][s] += 1
                o["sem"] = dsems[q][s]
                o["val"] = 16 * duse[q][s]
                o["prev"] = 16 * (duse[q][s] - 1)
            else:
                e = o["eng"]
                o["inc"] = o["idx"] in needed
                if o["inc"]:
                    tick[e] += 1
                o["sem"] = sems[e]
                o["val"] = tick[e]
        streams = {e: [] for e in ("pe", "act", "dve", "pool", "sp")}
        for o in ops:
            streams[o["eng"]].append(o)
        final_dma = [o for o in ops if o["dma"] and not o["cc"]]

        def run_stream(ename, eng):
            waited = {}
            for o in streams[ename]:
                waits = {}
                for d, raw in o["deps"].items():
                    do = ops[d]
                    if skip_edge(o, do, raw):
                        continue
                    key = id(do["sem"])
                    if waits.get(key, (None, -1))[1] < do["val"]:
                        waits[key] = (do["sem"], do["val"])
                if o["dma"] and o["prev"] > 0:
                    key = id(o["sem"])
                    if waits.get(key, (None, -1))[1] < o["prev"]:
                        waits[key] = (o["sem"], o["prev"])
                for key, (s, v) in waits.items():
                    if waited.get(key, -1) >= v:
                        continue
                    eng.wait_ge(s, v)
                    waited[key] = v
                ins = o["emit"](eng)
                if o["cc"]:
                    ins.then_inc(o["sem"], 1)
                elif o["dma"]:
                    ins.then_inc(o["sem"], 16)
                elif o["inc"]:
                    ins.then_inc(o["sem"], 1)
            if ename == "sp":
                last = {}
                for o in final_dma:
                    k = id(o["sem"])
                    last[k] = (o["sem"], max(o["val"], last.get(k, (None, 0))[1]))
                for k, (s, v) in last.items():
                    eng.wait_ge(s, v)

        with nc.Block() as block:
            @block.tensor
            def _(e):
                run_stream("pe", e)

            @block.scalar
            def _(e):
                run_stream("act", e)

            @block.vector
            def _(e):
                run_stream("dve", e)

            @block.gpsimd
            def _(e):
                run_stream("pool", e)

            @block.sync
            def _(e):
                run_stream("sp", e)
        return dict(n_ops=len(ops), ticks=dict(tick), dmas=dict(dcount))


PC_NG = 0
PC_HSW = 8
PC_HSB = 44
PC_HYB = 56
PC_LCW = 60
PC_LCB = 76
PC_LBA = 80
PC_LBX = 88
PC_LAM = 96
PC_QG = 104
PC_KG = 105
PC_B1 = 106
PC_B2 = 107
PC_F0 = 108
PC_F1 = 109
PC_N = 112


def build_program():
    nc = bass.Bass("TRN2", target_bir_lowering=False)

    def din(name, shape, dt=F32):
        return nc.dram_tensor(name, list(shape), dt, kind="ExternalInput").ap()

    def dout(name, shape):
        return nc.dram_tensor(name, list(shape), F32, kind="ExternalOutput").ap()

    xg_d = [din("xp", [1024, D]), din("xs", [1024, D])]
    ckT_d = din("ckT", [2, 64, 512])
    cv_d = din("cv", [2, 512, 64])
    wins_d = din("w_in_s", [2, D, 3 * 512])
    wouts_d = din("w_out_s", [2, 1536, D])
    pps_d = din("pps", [2, 128, PC_N])
    bds_d = din("bds", [2, 128, 16 * 128])
    w3ss_d = din("w3ss", [2, 64, 256])
    decs_d = din("decs", [2, 256])
    ccin_d = [[nc.dram_tensor("ccin%d_%d" % (i, t), [128, 1024], BF16) for t in range(3)] for i in range(2)]
    ccout_d = [[nc.dram_tensor("ccout%d_%d" % (i, t), [512, 1024], BF16) for t in range(3)] for i in range(2)]
    st_d = din("st", [128, 16])
    cp_d = din("cpack", [128, 16])
    wada_d = din("w_ada", [2, D, 3072])
    bada_d = din("b_ada", [2, 3072])
    win_d = din("w_in_r", [2, D, NG * 512])
    wout_d = din("w_out", [2, 1536, D])
    pp_d = din("ppack", [2, 128, PC_N])
    bd_d = din("lru_bd", [2, 128, 16 * 128])
    w1_d = din("w1", [2, 33, 64])
    w2_d = din("w2", [2, 64, 64])
    w3_d = din("w3r", [2, 64, 1024])
    dec_d = din("decr", [2, 1024])
    fg_d = din("final_g", [D])
    ident_d = din("ident", [128, 128], BF16)
    psw_d = din("psw", [128, 128], BF16)
    bones_d = din("bones", [128, 128], BF16)
    ft_d = {256: din("ft256", [33, 256]), 1024: din("ft1024", [33, 1024])}
    tc_d = {256: din("tc256", [128, 2]), 1024: din("tc1024", [128, 8])}
    fwd_d = {256: din("fwd256", [256, 512], BF16), 1024: din("fwd1024", [1024, 2048], BF16)}
    inv_d = {256: din("inv256", [512, 256], BF16), 1024: din("inv1024", [2048, 1024], BF16)}
    cos_d = din("ropec", [128, 1024], BF16)
    sin_d = din("ropes", [128, 1024], BF16)

    yg_d = [dout("yp", [1024, D]), dout("ys", [1024, D])]
    nk_d = dout("nk", [2, 1024, 128])
    nv_d = dout("nv", [2, 1024, 128])
    nl_d = dout("nl", [2, 8, 512])

    S = Sched(nc)

    def sb(name, shape, dt=F32):
        return nc.alloc_sbuf_tensor("s_" + name, list(shape), dt)

    x_t = sb("x_t", [128, 8, D])
    hT = sb("hT", [128, 8, 1024], BF16)
    wbuf = [sb("wbuf%d" % i, [128, 8, 512], BF16) for i in range(2)]
    mixT = sb("mixT", [128, 12, 1024], BF16)
    gate_bc = sb("gate_bc", [128, D])
    fg_bc = gate_bc
    gaterow = sb("gaterow", [33, 2, 1024], BF16)
    ABall = sb("ABall", [128, 4, 16])
    ident = sb("ident", [128, 128], BF16)
    psw = sb("psw", [128, 128], BF16)
    bones = sb("bones", [128, 128], BF16)
    onesb = sb("onesb", [128, 128], BF16)
    onesf = sb("onesf", [128, 128])
    pp = sb("pp", [128, PC_N])
    small = sb("small", [128, 64])
    ssq = sb("ssq", [128, 8])
    rstd = sb("rstd", [128, 8])
    csb = sb("csb", [128, 8, 64], BF16)
    cpk = sb("cpk", [128, 16])
    h0t = sb("h0t", [128, 16])
    bd = sb("bd", [128, 16, 128], BF16)
    w1s = sb("w1s", [33, 64])
    w2s = sb("w2s", [64, 64])
    w3s = sb("w3s", [64, 512])
    decbc = sb("decbc", [128, 512])
    tcol = sb("tcol", [128, 8])
    ntcol = sb("ntcol", [128, 8])
    ar_a = sb("ar_a", [128, 8192])
    ar_b = sb("ar_b", [128, 4096])
    ar_c = sb("ar_c", [128, 4096])
    ar_d = sb("ar_d", [128, 3072])
    PT = ar_a[:, 0:6144].bitcast(BF16).rearrange("p (s k n) -> p s k n", s=2, k=12)
    qT = ar_b[:, 0:2048].bitcast(BF16).rearrange("p (c n) -> p c n", c=4)
    kkT = ar_b[:, 2048:3584].bitcast(BF16).rearrange("p (g n) -> p g n", g=2)
    vv = ar_c[:, 0:1536].bitcast(BF16).rearrange("p (t n) -> p t n", t=12)
    sga = ar_c[:, 1536:3584].bitcast(BF16).rearrange("p (c n) -> p c n", c=4)
    ko = ar_d[:, 0:1024].rearrange("p (t n) -> p t n", t=8)
    nvo = ar_d[:, 1024:2048].rearrange("p (t n) -> p t n", t=8)
    rcp = ar_d[:, 2048:2560]
    ytmp = ar_d[:, 2560:3072]
    xc = ar_a[:, 0:4096].rearrange("p (c n) -> p c n", c=4)
    xcb = ar_a[:, 4096:6144].bitcast(BF16).rearrange("p (c n) -> p c n", c=4)
    avs = [ar_b[:, 0:1024], ar_a[:, 6144:7168]]
    bvs = [ar_b[:, 1024:2048], ar_a[:, 7168:8192]]
    hsf = ar_b[:, 2048:3072]
    rt = ar_b[:, 3072:4096]
    it = ar_d[:, 2048:3072]
    yacc = ar_c[:, 0:4096].rearrange("p (c n) -> p c n", c=4)
    sgl = ar_d[:, 0:2048].bitcast(BF16).rearrange("p (c n) -> p c n", c=4)
    HL = small[:, 24:56].rearrange("p (c q) -> p c q", c=4)
    dft = ar_a[:, :].bitcast(BF16)
    x0c = ar_b[:, 0:2048].rearrange("p (c n) -> p c n", c=2)
    zT = ar_b[:, 2048:3072].bitcast(BF16).rearrange("p (c n) -> p c n", c=2)
    ztok = ar_b[:, 3072:4096].bitcast(BF16).rearrange("p (t n) -> p t n", t=8)
    Ypk = ar_c[:, 0:2048].bitcast(BF16)
    filt = ar_c[:, 2048:4096].bitcast(BF16).rearrange("p (t n) -> p t n", t=8)
    sgh = ar_d[:, 0:1024].bitcast(BF16).rearrange("p (c n) -> p c n", c=2)
    Hs = ar_d[:, 1024:1536].rearrange("p (a n) -> p a n", a=2)
    Zs = ar_d[:, 1536:2048]
    rnb = ar_d[:, 2048:2304]
    hyt1 = ar_d[:, 2304:2560]
    hyt2 = ar_d[:, 2560:2816]
    hyt3 = ar_d[:, 2816:3072]
    ust = sb("ust", [128, 2, 1024])
    cvt = ar_c[:, 0:1024]
    xn = ust[:, :, :].rearrange("p a n -> p (a n)").bitcast(BF16).rearrange("p (j n) -> p j n", j=4)
    XSW = 1056
    xsb = [ust[:, 0, 0:528].bitcast(BF16), ust[:, 0, 528:1024].bitcast(BF16)[:, 0:0] if False else ust[:, 1, 0:528].bitcast(BF16)]
    dgt = ust[:, 1, 528:784].bitcast(BF16).rearrange("p (j c) -> p j c", j=4)
    ft = ust[0:33, 1, :]
    sc1 = sb("sc1", [128, 512])
    sc2 = sb("sc2", [128, 512])
    sc3 = sb("sc3", [128, 512])
    sqb = sc3[:, 0:256].bitcast(BF16)
    qnb = sb("qnb", [128, 512], BF16)
    hdn1 = sb("hdn1", [64, 1024])
    tA = sc1[0:64, :]
    tI = sc2[0:64, :].bitcast(I32)
    tF = sc3[0:64, :]
    wada = wbuf
    wo = [wbuf[i][:, :, :].rearrange("p k c -> p (k c)")[:, 0:3072].rearrange("p (k c) -> p k c", k=12) for i in range(2)]
    modtmp = ar_c[0:33, 0:3072]
    for n_, ro in (("badab", ("ar_a", "ada")), ("modtmp", ("ar_c", "ada")),
                   ("PT", ("ar_a", "attn")), ("xc", ("ar_a", "lru")), ("xcb", ("ar_a", "lru")), ("dft", ("ar_a", "hy")),
                   ("qT", ("ar_b", "attn")), ("kk", ("ar_b", "attn")),
                   ("av", ("ar_b", "lru")), ("bv", ("ar_b", "lru")), ("avx", ("ar_a", "lru")), ("hsf", ("ar_b", "lru")), ("rt", ("ar_b", "lru")), ("it", ("ar_d", "lru")),
                   ("x0c", ("ar_b", "hy")), ("zT", ("ar_b", "hy")), ("ztok", ("ar_b", "hy")),
                   ("vv", ("ar_c", "attn")), ("sga", ("ar_c", "attn")), ("yacc", ("ar_c", "lru")), ("Y", ("ar_c", "hy")), ("filt", ("ar_c", "hy")),
                   ("ko", ("ar_d", "attn")), ("nvo", ("ar_d", "attn")), ("rcp", ("ar_d", "attn")), ("ytmp", ("ar_d", "attn")),
                   ("sgl", ("ar_d", "lru")),
                   ("sgh", ("ar_d", "hy")), ("Hs", ("ar_d", "hy")), ("Zs", ("ar_d", "hy")), ("rnb", ("ar_d", "hy")),
                   ("hyt1", ("ar_d", "hy")), ("hyt2", ("ar_d", "hy")), ("hyt3", ("ar_d", "hy")),
                   ("xn", ("r_ust", "xn")), ("ust", ("r_ust", "ust"))):
        S.alias[n_] = ro

    banks = [nc.alloc_psum_tensor("pb%d" % i, [128, 512], F32) for i in range(4)]
    pst = nc.alloc_psum_tensor("pst", [128, 1024], F32)
    pab = nc.alloc_psum_tensor("pab", [128, 1024], F32)
    banks = [bk[:, :] for bk in banks] + [pst[:, 0:512], pst[:, 512:1024], pab[:, 0:512], pab[:, 512:1024]]
    rot = {"mm": [0, 1, 2], "tr": [3, 7], "st": [4, 5], "A": [6, 0], "B": [7, 1]}
    rpos = {k: 0 for k in rot}

    def bank(cls):
        b = rot[cls][rpos[cls] % len(rot[cls])]
        rpos[cls] += 1
        return b

    def PB(b):
        return banks[b]

    def MM(out, lhsT, rhs, start, stop, reads, writes):
        S.op("pe", lambda e: e.matmul(out, lhsT=lhsT, rhs=rhs, start=start, stop=stop), reads, writes)

    def TR(out, in_, reads, writes):
        S.op("pe", lambda e: e.transpose(out=out, in_=in_, identity=ident[:]), list(reads) + ["ident"], writes)

    def ACT(out, in_, func, reads, writes, scale=1.0, bias=None, accum=None):
        kw = {}
        if bias is not None:
            kw["bias"] = bias
        if accum is not None:
            kw["accum_out"] = accum
        S.op("act", lambda e: e.activation(out=out, in_=in_, func=func, scale=scale, **kw), reads, writes)

    def TS(eng, out, in0, s1, s2, op0, op1, reads, writes):
        if s2 is None:
            S.op(eng, lambda e: e.tensor_scalar(out=out, in0=in0, scalar1=s1, scalar2=None, op0=op0), reads, writes)
        else:
            S.op(eng, lambda e: e.tensor_scalar(out=out, in0=in0, scalar1=s1, scalar2=s2, op0=op0, op1=op1), reads, writes)

    def TT(eng, out, in0, in1, op, reads, writes):
        S.op(eng, lambda e: e.tensor_tensor(out=out, in0=in0, in1=in1, op=op), reads, writes)

    def STT(eng, out, in0, scalar, in1, op0, op1, reads, writes):
        eng = "dve"
        S.op(eng, lambda e: e.scalar_tensor_tensor(out=out, in0=in0, scalar=scalar, in1=in1, op0=op0, op1=op1), reads, writes)

    def CP(eng, out, in_, reads, writes):
        if eng == "act":
            S.op(eng, lambda e: e.activation(out=out, in_=in_, func=AF.Identity), reads, writes)
        else:
            S.op(eng, lambda e: e.tensor_copy(out=out, in_=in_), reads, writes)

    def MEMSET(eng, ap, val, writes):
        S.op(eng, lambda e: e.memset(ap, val), (), writes)

    def DMA(out, in_, reads, writes, q="sp"):
        S.op(q, lambda e: e.dma_start(out=out, in_=in_), reads, writes, dma=True)

    epsc = small[:, 0:1]
    onec = small[:, 1:2]
    lnhalf = small[:, 2:3]
    badab = ar_a[0:33, 0:3072]
    def startup():
        DMA(ident[:], ident_d, (), ["ident"])
        DMA(psw[:], psw_d, (), ["psw"])
        DMA(bones[:], bones_d, (), ["bones"])
        DMA(cpk[:], cp_d, (), ["cpk"])
        DMA(h0t[:], st_d, (), ["h0t"])
        MEMSET("dve", onesb[:], 1.0, ["onesb"])
        MEMSET("dve", onesf[:], 1.0, ["onesf"])
        MEMSET("dve", csb[:], 0.0, ["csb"])
        MEMSET("dve", small[:], 0.0, ["small"])
        MEMSET("dve", small[:, 0:1], EPS, ["small"])
        MEMSET("dve", small[:, 1:2], 1.0, ["small"])
        MEMSET("dve", small[:, 2:3], math.log(0.5), ["small"])

        CK(-5)
        ACT(csb[:, :, ::32], cpk[:, :].rearrange("p (k v) -> p k v", v=2), AF.Silu, ["cpk", "csb"], ["csb"])
        CK(-4)
        DMA(pp[:], pp_d[0], (), ["pp"])
        for l in range(2):
            DMA(badab, bada_d[l].partition_broadcast(33), (), ["badab"])
            if l == 1:
                DMA(pp[:], pp_d[1], (), ["pp"])
            for j in range(6):
                slot = j % 2
                DMA(wada[slot][:], wada_d[l].rearrange("(k p) c -> p k c", p=128)[:, :, j * 512:(j + 1) * 512], (), [("win", slot)], q="pool")
                b = bank("mm")
                for kc in range(8):
                    MM(PB(b)[0:33, :], csb[:, kc, 0:33], wada[slot][:, kc, :], kc == 0, kc == 7,
                       [("win", slot), "csb"], [("ps", b)])
                TT("dve", modtmp[:, j * 512:(j + 1) * 512], PB(b)[0:33, :], badab[:, j * 512:(j + 1) * 512], ALU.add,
                   [("ps", b), "badab"], ["modtmp"])
            CK(-30 + l)
            CP("dve", gaterow[:, l, :], modtmp[:, 2048:3072], ["modtmp"], [("gaterow", l)])
            CK(-20 + l)
            for gi in range(2):
                r0 = 32 * gi
                b = bank("mm")
                for cc in range(16):
                    MM(PB(b)[:, cc:cc + 1], modtmp[r0:r0 + 1, cc * 128:(cc + 1) * 128], onesf[r0:r0 + 1, 0:1], True, True,
                       ["modtmp", "onesf"], [("ps", b)])
                ABv = ABall[:, gi * 2 + l, :]
                CP("dve", ABv[:, 8:16], PB(b)[:, 0:8], [("ps", b)], [("AB", gi, l)])
                STT("dve", ABv[:, 0:8], PB(b)[:, 8:16], 1.0, pp[:, PC_NG:PC_NG + 8], ALU.add, ALU.mult, [("ps", b), "pp"], [("AB", gi, l)])

    def layer_pass(gi, l, nseq, L):
        latent = gi == 1
        r0 = 32 * gi
        NQC, NKV, NLC, HH, NC2 = (4, 2, 4, 2, 2) if gi == 0 else (1, 1, 1, 1, 1)
        CW = NC2 * 128
        VW = NKV * 128
        win_src = win_d if gi == 0 else wins_d
        NGP = 9 if gi == 0 else 3
        if gi == 0:
            LOC = dict(q=lambda cc: (0, cc * 128), kk=lambda g: (1, g * 128), v=(1, 256), ga=lambda cc: (2, cc * 128),
                       lx=lambda cc: (3, cc * 128), lg=lambda cc: (4, cc * 128),
                       x0=lambda hh, c2: (5 + 2 * hh, c2 * 128), x1=lambda hh, c2: (5 + 2 * hh, 256 + c2 * 128),
                       hv=lambda hh, c2: (6 + 2 * hh, c2 * 128), hg=lambda hh, c2: (6 + 2 * hh, 256 + c2 * 128))
            PF = dict(start=[0, 1], after_q=[2], after_k=[3], lru_start=[4], after_lx=[5], hh0=[6], after_hx0=[7], hh1=[8])
        else:
            LOC = dict(q=lambda cc: (0, 0), kk=lambda g: (0, 128), v=(0, 256), ga=lambda cc: (0, 384),
                       lx=lambda cc: (1, 0), lg=lambda cc: (1, 128), x0=lambda hh, c2: (1, 256), x1=lambda hh, c2: (1, 384),
                       hv=lambda hh, c2: (2, 0), hg=lambda hh, c2: (2, 128))
            PF = dict(start=[0, 1], lru_start=[2])
        NQ = min(L, 512)
        nqb = L // NQ
        ntl = L // 128
        Lctx = 512 if latent else 0
        nkt = (Lctx + L) // 128
        nf = L // 128

        DMA(pp[:], (pp_d if gi == 0 else pps_d)[l], (), ["pp"])
        DMA(bd[:], (bd_d if gi == 0 else bds_d)[l].rearrange("p (n c) -> p n c", n=16), (), ["bd"], q="pool")
        DMA(w1s[:], w1_d[l], (), ["w1s"])
        DMA(w2s[:], w2_d[l], (), ["w2s"])
        DMA(tcol[:, 0:nf], tc_d[L], (), ["tcol"])
        TS("dve", ntcol[:, 0:nf], tcol[:, 0:nf], -1.0, None, ALU.mult, None, ["tcol"], ["ntcol"])

        AB = ABall[:, gi * 2 + l, :]
        for h in range(2):
            b = bank("mm")
            MM(PB(b)[:, :], onesb[r0:r0 + 1, 0:128], gaterow[r0:r0 + 1, l, h * 512:(h + 1) * 512], True, True,
               [("gaterow", l), "onesb"], [("ps", b)])
            CP("act", gate_bc[:, h * 512:(h + 1) * 512], PB(b)[:, :], [("ps", b)], [("gate", h)])

        for hb in range(2):
            for jj in range(4):
                j = hb * 4 + jj
                ACT(xn[:, jj, :], x_t[:, j, :], AF.Square, [("x", j)], [("xn", jj), ("ssq", hb)], accum=ssq[:, j:j + 1])
            ACT(rstd[:, hb * 4:hb * 4 + 4], ssq[:, hb * 4:hb * 4 + 4], AF.Ln, [("ssq", hb), "small"], [("rstd", hb)], scale=1.0 / D, bias=epsc)
            ACT(rstd[:, hb * 4:hb * 4 + 4], rstd[:, hb * 4:hb * 4 + 4], AF.Exp, [("rstd", hb)], [("rstd", hb)], scale=-0.5)
            for jj in range(4):
                j = hb * 4 + jj
                TS("dve", xn[:, jj, :], x_t[:, j, :], rstd[:, j:j + 1], None, ALU.mult, None, [("x", j), ("rstd", hb)], [("xn", jj)])
            for ccp in range(4):
                b = bank("tr")
                pbv = PB(b)[:, :].bitcast(BF16).rearrange("p (c n) -> p c n", c=2)
                for ci in range(2):
                    cc = ccp * 2 + ci
                    for jj in range(4):
                        TR(pbv[:, ci, jj * 128:(jj + 1) * 128], xn[:, jj, cc * 128:(cc + 1) * 128], [("xn", jj)], [("ps", b)])
                for ci in range(2):
                    cc = ccp * 2 + ci
                    TS("dve", hT[:, cc, hb * 512:(hb + 1) * 512], pbv[:, ci, :], AB[:, cc:cc + 1], AB[:, 8 + cc:9 + cc], ALU.mult, ALU.add,
                       [("ps", b), ("AB", gi, l)], [("hT", cc, hb)])

        CK(2 + 10 * (gi * 2 + l))
        loaded = set()

        def ensure(g):
            if g in loaded or g >= NGP:
                return
            loaded.add(g)
            slot = g % 2
            DMA(wbuf[slot][:], win_src[l].rearrange("(k p) c -> p k c", p=128)[:, :, g * 512:(g + 1) * 512], (), [("win", slot)], q="pool")

        def pf(tag):
            for g_ in PF.get(tag, []):
                ensure(g_)

        def inproj(loc, hf):
            g_, c0 = loc
            ensure(g_)
            slot = g_ % 2
            b = bank("mm")
            for kc in range(8):
                MM(PB(b)[:, :], wbuf[slot][:, kc, c0:c0 + 128], hT[:, kc, hf * 512:(hf + 1) * 512], kc == 0, kc == 7,
                   [("win", slot), ("hT", kc, hf)], [("ps", b)])
            return b

        def headnorm(b, gcol, dst, dkeys, hf):
            ACT(sqb, PB(b)[:, :], AF.Square, [("ps", b)], ["sc3"])
            bs = bank("st")
            MM(PB(bs)[:, :], bones[:], sqb, True, True, ["bones", "sc3"], [("ps", bs)])
            ACT(sc1[:], PB(bs)[:, :], AF.Ln, [("ps", bs), "small"], ["sc1"], scale=1.0 / 64, bias=epsc)
            ACT(sc1[:], sc1[:], AF.Exp, ["sc1"], ["sc1"], scale=-0.5)
            if not latent:
                STT("dve", dst, PB(b)[:, :], pp[:, gcol:gcol + 1], sc1[:], ALU.mult, ALU.mult, [("ps", b), "pp", "sc1"], dkeys)
            else:
                STT("dve", qnb[:], PB(b)[:, :], pp[:, gcol:gcol + 1], sc1[:], ALU.mult, ALU.mult, [("ps", b), "pp", "sc1"], ["qnb"])
                bs2 = bank("st")
                MM(PB(bs2)[:, :], psw[:], qnb[:], True, True, ["psw", "qnb"], [("ps", bs2)])
                TT("dve", sc2[:], qnb[:], ropec[:, hf * 512:(hf + 1) * 512], ALU.mult, ["qnb", "ropec"], ["sc2"])
                TT("dve", sc3[:], PB(bs2)[:, :], ropes[:, hf * 512:(hf + 1) * 512], ALU.mult, [("ps", bs2), "ropes"], ["sc3"])
                TT("pool", dst, sc2[:], sc3[:], ALU.add, ["sc2", "sc3"], dkeys)

        def gather(t, lslot):
            lk = [("mix", lslot, 0), ("mix", lslot, 1)]
            ok = [("mix", 4 * t + r_, h_) for r_ in range(4) for h_ in range(2)]
            DMA(ccin_d[l][t].ap(), mixT[:, lslot, :], lk, [("ccin", l, t)])
            S.op("pool", (lambda l_, t_: lambda e: e.collective_compute("AllGather", ALU.bypass, replica_groups=[[0, 2, 4, 6], [1, 3, 5, 7]],
                                                                         ins=[ccin_d[l_][t_].ap()], outs=[ccout_d[l_][t_].ap()]))(l, t),
                 [("ccin", l, t)], [("ccout", l, t)], cc=True)
            DMA(mixT[:, 4 * t:4 * t + 4, :], ccout_d[l][t].ap().rearrange("(c p) n -> p c n", p=128), [("ccout", l, t)], ok)

        xst = {"n": 0}
        Lp = L + 4

        def zero_pads():
            for xb in xsb:
                x3 = xb[:, 0:nseq * Lp].rearrange("p (s n) -> p s n", s=nseq)
                MEMSET("pool", x3[:, :, 0:1], 0.0, [("ust", 0), ("ust", 1)])
                MEMSET("pool", x3[:, :, L + 1:L + 4], 0.0, [("ust", 0), ("ust", 1)])

        def proj_conv(loc, ntaps, wcol, evac):
            slot = xst["n"] % 2
            xst["n"] += 1
            xb = xsb[slot]
            x3 = xb[:, 0:nseq * Lp].rearrange("p (s n) -> p s n", s=nseq)
            for hf in range(2):
                b = inproj(loc, hf)
                if L == 256:
                    CP("act", x3[:, 2 * hf:2 * hf + 2, 1:1 + L], PB(b)[:, :].rearrange("p (s n) -> p s n", s=2), [("ps", b)], [("ust", slot)])
                else:
                    CP("act", x3[:, 0, 1 + hf * 512:1 + (hf + 1) * 512], PB(b)[:, :], [("ps", b)], [("ust", slot)])
            for j in range(ntaps):
                TS("dve", dgt[:, j, :], ident[:], wcol(j), None, ALU.mult, None, ["ident", "pp", ("ust", 1)], ["dg"])
            for hf in range(2):
                b = bank("st")
                if L == 256:
                    for si in range(2):
                        sq = 2 * hf + si
                        for j in range(ntaps):
                            MM(PB(b)[:, si * 256:(si + 1) * 256], dgt[:, j, :], x3[:, sq, j:j + L], j == 0, j == ntaps - 1,
                               ["dg", ("ust", slot)], [("ps", b)])
                else:
                    for j in range(ntaps):
                        MM(PB(b)[:, :], dgt[:, j, :], x3[:, 0, hf * 512 + j:hf * 512 + j + 512], j == 0, j == ntaps - 1,
                           ["dg", ("ust", slot)], [("ps", b)])
                evac(hf, b)

        pf("start")
        if latent:
            for dup in range(2):
                DMA(kkT[dup * 64:(dup + 1) * 64, 0, 0:512], ckT_d[l], (), [("kk", 0, 0)], q="pool")
                DMA(vv[:, 0:4, dup * 64:(dup + 1) * 64], cv_d[l].rearrange("(t p) d -> p t d", p=128), (), [("vv", t) for t in range(4)], q="pool")
        for cc in range(NQC):
            for hf in range(2):
                b = inproj(LOC["q"](cc), hf)
                headnorm(b, PC_QG, qT[:, cc, hf * 512:(hf + 1) * 512], [("qT", cc, hf)], hf)
        CK(2.1 + 10 * (gi * 2 + l))
        pf("after_q")
        for g in range(NKV):
            for hf in range(2):
                b = inproj(LOC["kk"](g), hf)
                headnorm(b, PC_KG, kkT[:, g, Lctx + hf * 512:Lctx + (hf + 1) * 512], [("kk", g, 1 + hf)], hf)
        CK(2.2 + 10 * (gi * 2 + l))
        vt0 = Lctx // 128
        vg_, vcol = LOC["v"]
        ensure(vg_)
        s_k = vg_ % 2
        for j in range(8):
            b = bank("mm")
            for kc in range(8):
                MM(PB(b)[:, 0:VW], hT[:, kc, j * 128:(j + 1) * 128], wbuf[s_k][:, kc, vcol:vcol + VW], kc == 0, kc == 7,
                   [("win", s_k), ("hT", kc, j // 4)], [("ps", b)])
            CP("act", vv[:, vt0 + j, 0:VW], PB(b)[:, 0:VW], [("ps", b)], [("vv", vt0 + j)])
            if not latent and os.environ.get("KNVO", "1") == "1":
                CP("act", nvo[:, j, :].rearrange("p (k d) -> p k d", k=2),
                   PB(b)[:, 0:256].rearrange("p (k u d) -> p k u d", k=2, u=2)[:, :, 0, :], [("ps", b)], ["nvo"])
        CK(2.3 + 10 * (gi * 2 + l))
        if not latent:
            DMA(nv_d[l].rearrange("(t p) c -> p t c", p=128), nvo[:, :, :], ["nvo"], ())
            for j in range(8):
                b = bank("tr")
                pbv = PB(b)[:, :].bitcast(BF16)
                for g in range(2):
                    TR(pbv[:, g * 128:(g + 1) * 128], kkT[:, g, j * 128:(j + 1) * 128], [("kk", g, 1 + j // 4)], [("ps", b)])
                CP("act", ko[:, j, :].rearrange("p (g d) -> p g d", g=2),
                   pbv[:, 0:256].rearrange("p (g n) -> p g n", g=2)[:, :, 0:64], [("ps", b)], ["ko"])
            DMA(nk_d[l].rearrange("(t p) c -> p t c", p=128), ko[:, :, :], ["ko"], ())
        CK(2.4 + 10 * (gi * 2 + l))
        pf("after_k")
        for cc in range(NQC):
            for hf in range(2):
                b = inproj(LOC["ga"](cc), hf)
                ACT(sga[:, cc, hf * 512:(hf + 1) * 512], PB(b)[:, :], AF.Silu, [("ps", b)], [("sga", cc, hf)])
        CK(2.5 + 10 * (gi * 2 + l))
        i2p = 1.0 / (2 * math.pi)
        TS("dve", small[0:64, 16:17], pp[0:64, PC_F0:PC_F0 + 1], i2p, None, ALU.mult, None, ["pp", "small"], ["small"])
        TT("dve", small[0:64, 17:18], small[0:64, 16:17], pp[0:64, PC_B1:PC_B1 + 1], ALU.mult, ["pp", "small"], ["small"])
        TS("dve", small[0:64, 18:19], pp[0:64, PC_F1:PC_F1 + 1], i2p, None, ALU.mult, None, ["pp", "small"], ["small"])
        TT("dve", small[0:64, 19:20], small[0:64, 18:19], pp[0:64, PC_B2:PC_B2 + 1], ALU.mult, ["pp", "small"], ["small"])
        DMA(ft[:, 0:L], ft_d[L], (), [("ust", 1)])
        hdn0 = ust[0:64, 0, :]
        for stage in range(2):
            for blk in range(nqb):
                n0 = blk * NQ
                b = bank("mm")
                if stage == 0:
                    MM(PB(b)[0:64, 0:NQ], w1s[0:33, :], ft[0:33, n0:n0 + NQ], True, True, ["w1s", ("ust", 1)], [("ps", b)])
                else:
                    MM(PB(b)[0:64, 0:NQ], w2s[0:64, :], hdn0[:, n0:n0 + NQ], True, True, ["w2s", ("ust", 0)], [("ps", b)])
                c1 = 16 + stage * 2
                TS("dve", tA[:, 0:NQ], PB(b)[0:64, 0:NQ], small[0:64, c1:c1 + 1], small[0:64, c1 + 1:c1 + 2], ALU.mult, ALU.add, [("ps", b), "small"], ["sc1"])
                CP("dve", tI[:, 0:NQ], tA[:, 0:NQ], ["sc1"], ["sc2"])
                CP("dve", tF[:, 0:NQ], tI[:, 0:NQ], ["sc2"], ["sc3"])
                TT("dve", tA[:, 0:NQ], tA[:, 0:NQ], tF[:, 0:NQ], ALU.subtract, ["sc1", "sc3"], ["sc1"])
                if stage == 0:
                    ACT(hdn0[:, n0:n0 + NQ], tA[:, 0:NQ], AF.Sin, ["sc1"], [("ust", 0)], scale=2 * math.pi)
                else:
                    ACT(hdn1[:, n0:n0 + NQ], tA[:, 0:NQ], AF.Sin, ["sc1"], ["hdn1"], scale=2 * math.pi)

        items = [(s_, h_, qb_) for s_ in range(nseq) for h_ in range(2 * NQC) for qb_ in range(nqb)]

        def att_s1(k):
            s, h, qb = items[k]
            g = (h // 4) if NKV == 2 else 0
            cc = h // 2
            hp = (h % 2) * 64
            q0 = s * L + qb * NQ
            qhf = q0 // 512
            pslot = k % 2
            for kt in range(nkt):
                koff = (kt * 128) if latent else (s * L + kt * 128)
                kkey = ("kk", g, 0) if (latent and kt < 4) else ("kk", g, 1 + (koff - Lctx) // 512)
                bs = bank("st")
                MM(PB(bs)[:, 0:NQ], kkT[hp:hp + 64, g, koff:koff + 128], qT[hp:hp + 64, cc, q0:q0 + NQ], True, True,
                   [kkey, ("qT", cc, qhf)], [("ps", bs)])
                ACT(PT[:, pslot, kt, 0:NQ], PB(bs)[:, 0:NQ], AF.Exp, [("ps", bs)], [("PT", pslot, kt)], scale=0.125)

        def att_s2(k):
            s, h, qb = items[k]
            g = (h // 4) if NKV == 2 else 0
            cc = h // 2
            hp = (h % 2) * 64
            q0 = s * L + qb * NQ
            qhf = q0 // 512
            pslot = k % 2
            bA = bank("A")
            bB = bank("B")
            for kt in range(nkt):
                vt = kt if latent else (s * ntl + kt)
                MM(PB(bA)[:, 0:NQ], vv[:, vt, g * 128:(g + 1) * 128], PT[:, pslot, kt, 0:NQ], kt == 0, kt == nkt - 1,
                   [("vv", vt), ("PT", pslot, kt)], [("ps", bA)])
            for kt in range(nkt):
                MM(PB(bB)[:, 0:NQ], onesb[:], PT[:, pslot, kt, 0:NQ], kt == 0, kt == nkt - 1,
                   ["onesb", ("PT", pslot, kt)], [("ps", bB)])
            ACT(rcp[hp:hp + 64, 0:NQ], PB(bB)[hp:hp + 64, 0:NQ], AF.Ln, [("ps", bB)], [("rcp", hp)])
            ACT(rcp[hp:hp + 64, 0:NQ], rcp[hp:hp + 64, 0:NQ], AF.Exp, [("rcp", hp)], [("rcp", hp)], scale=-1.0)
            TT("dve", ytmp[hp:hp + 64, 0:NQ], PB(bA)[hp:hp + 64, 0:NQ], rcp[hp:hp + 64, 0:NQ], ALU.mult,
               [("ps", bA), ("rcp", hp)], [("ytmp", hp)])
            amix = cc if gi == 0 else 4
            TT("pool", mixT[hp:hp + 64, amix, q0:q0 + NQ], ytmp[hp:hp + 64, 0:NQ], sga[hp:hp + 64, cc, q0:q0 + NQ], ALU.mult,
               [("ytmp", hp), ("sga", cc, qhf)], [("mix", amix, qhf)])

        att_s1(0)
        for k in range(len(items)):
            if k + 1 < len(items):
                att_s1(k + 1)
            att_s2(k)

        if gi == 1:
            gather(0, 4)
        CK(3 + 10 * (gi * 2 + l))
        def dwconv(src, dst, wcol0, nt, bcol, left, eng="pool", rk=(), wk=()):
            pass

        pf("lru_start")
        ACT(small[:, 8:16], pp[:, PC_LAM:PC_LAM + 8], AF.Exp, ["pp", "small"], ["small"], scale=-1.0)
        ACT(small[:, 8:16], small[:, 8:16], AF.Ln, ["small"], ["small"], scale=1.0, bias=onec)
        TS("dve", small[:, 8:16], small[:, 8:16], -8.0, None, ALU.mult, None, ["small"], ["small"])
        nsp = small[:, 8:16]
        zero_pads()
        for cc in range(NLC):
            def ev_lx(hf, b, cc=cc):
                sl = slice(hf * 512, (hf + 1) * 512)
                TS("dve", xc[:, cc, sl], PB(b)[:, :], pp[:, PC_LCB + cc:PC_LCB + cc + 1], None, ALU.add, None, [("ps", b), "pp"], [("xc", cc)])
                CP("dve", xcb[:, cc, sl], xc[:, cc, sl], [("xc", cc)], [("xcb", cc)])
            proj_conv(LOC["lx"](cc), 4, (lambda j, cc=cc: pp[:, PC_LCW + j * 4 + cc:PC_LCW + j * 4 + cc + 1]), ev_lx)
        pf("after_lx")
        for cc in range(NLC):
            for hf in range(2):
                b = inproj(LOC["lg"](cc), hf)
                ACT(sgl[:, cc, hf * 512:(hf + 1) * 512], PB(b)[:, :], AF.Silu, [("ps", b)], [("sgl", cc, hf)])
        pf("hh0")
        for d in range(2):
            for cc in range(NLC):
                col = d * 4 + cc
                for hf in range(2):
                    sl = slice(hf * 512, (hf + 1) * 512)
                    MM(pst[:, sl], bd[:, (d * 2 + 0) * 4 + cc, :], xcb[:, cc, sl], True, True, ["bd", ("xcb", cc)], [("ps", 4 + hf)])
                for hf in range(2):
                    sl = slice(hf * 512, (hf + 1) * 512)
                    MM(pab[:, sl], bd[:, (d * 2 + 1) * 4 + cc, :], xcb[:, cc, sl], True, True, ["bd", ("xcb", cc)], [("ps", 6 + hf)])
                ACT(rt[:, :], pst[:, :], AF.Sigmoid, [("ps", 4), ("ps", 5), "pp"], ["rt"], bias=pp[:, PC_LBA + col:PC_LBA + col + 1])
                ACT(it[:, :], pab[:, :], AF.Sigmoid, [("ps", 6), ("ps", 7), "pp"], ["it"], bias=pp[:, PC_LBX + col:PC_LBX + col + 1])
                u = (d * NLC + cc) % 2
                av, bv = avs[u], bvs[u]
                ak, bk = [("av", u)], [("bv", u)]
                if u == 1:
                    ak = ak + ["avx"]
                ACT(av[:, :], rt[:, :], AF.Exp, ["rt", "small"], ak, scale=nsp[:, col:col + 1])
                TT("dve", bv[:, :], av[:, :], av[:, :], ALU.mult, ak, bk)
                ACT(bv[:, :], bv[:, :], AF.Ln, bk + ["small"], bk, scale=-1.0, bias=onec)
                ACT(bv[:, :], bv[:, :], AF.Exp, bk, bk, scale=0.5)
                TT("dve", bv[:, :], bv[:, :], it[:, :], ALU.mult, bk + ["it"], bk)
                TT("dve", bv[:, :], bv[:, :], xc[:, cc, :], ALU.mult, bk + [("xc", cc)], bk)
                for s in range(nseq):
                    t0 = s * L
                    if latent:
                        init = h0t[:, (l * 2 + d) * 4 + cc:(l * 2 + d) * 4 + cc + 1]
                    else:
                        init = 0.0
                    if d == 0:
                        o_, a_, b_ = yacc[:, cc, t0:t0 + L], av[:, t0:t0 + L], bv[:, t0:t0 + L]
                    else:
                        o_, a_, b_ = hsf[:, t0:t0 + L][:, ::-1], av[:, t0:t0 + L][:, ::-1], bv[:, t0:t0 + L][:, ::-1]
                    S.op("dve", (lambda o_, a_, b_, init: lambda e: e.tensor_tensor_scan(out=o_, data0=a_, data1=b_, initial=init, op0=ALU.mult, op1=ALU.add))(o_, a_, b_, init),
                         ak + bk + ["h0t"], [("yacc", cc)] if d == 0 else ["hsf"])
                if not latent:
                    for s in range(nseq):
                        t0 = s * L
                        if d == 0:
                            CP("pool", HL[:, cc, s * 2:s * 2 + 1], yacc[:, cc, t0 + L - 1:t0 + L], [("yacc", cc)], ["HL"])
                        else:
                            CP("pool", HL[:, cc, s * 2 + 1:s * 2 + 2], hsf[:, t0:t0 + 1], ["hsf"], ["HL"])
                if d == 1:
                    TT("dve", yacc[:, cc, :], yacc[:, cc, :], hsf[:, :], ALU.add, [("yacc", cc), "hsf"], [("yacc", cc)])
                    for hf in range(2):
                        sl = slice(hf * 512, (hf + 1) * 512)
                        lmix = (8 + cc) if gi == 0 else 5
                        TT("pool", mixT[:, lmix, sl], yacc[:, cc, sl], sgl[:, cc, sl], ALU.mult, [("yacc", cc), ("sgl", cc, hf)], [("mix", lmix, hf)])
        if not latent:
            for cc in range(NLC):
                DMA(nl_d[l].rearrange("q (c p) -> p c q", p=128)[:, cc, :], HL[:, cc, :], ["HL"], ())

        if gi == 1:
            gather(2, 5)
        CK(4 + 10 * (gi * 2 + l))
        fwdv = (lambda jt, c0: fwd256[:, jt, c0:c0 + 128]) if L == 256 else (lambda jt, c0: dft.rearrange("p (t r) -> p t r", t=8)[:, jt, c0:c0 + 128])
        invv = (lambda rc, t0, n: inv256[:, rc, t0:t0 + n]) if L == 256 else (lambda rc, t0, n: dft.rearrange("p (t r) -> p t r", t=16)[:, rc, t0:t0 + n])
        dkey = "ropec" if L == 256 else "dft"
        ikey = "ropes" if L == 256 else "dft"

        wb0v = wbuf[0][:, :, :].rearrange("p k c -> p (k c)").rearrange("p (k c) -> p k c", k=4)

        def load_wout():
            wov = wout_d[l].rearrange("(k p) c -> p k c", p=128)
            if gi == 0:
                for ch_ in range(2):
                    DMA(wo_full[:, :, ch_ * 512:(ch_ + 1) * 512], wov[:, :, ch_ * 512:(ch_ + 1) * 512], (), [("wof", ch_)], q="pool")
            else:
                for h4 in range(2):
                    DMA(hT[:, 4 * h4:4 * h4 + 4, :], wov[:, 4 * h4:4 * h4 + 4, :], (),
                        [("hT", k_, b_) for k_ in range(4 * h4, 4 * h4 + 4) for b_ in range(2)], q="pool")
                DMA(wb0v, wov[:, 8:12, :], (), [("win", 0)], q="pool")

        def wo_rhs(kc, ch):
            if gi == 0:
                return wo_full[:, kc, ch * 512:(ch + 1) * 512], ("wof", ch)
            if kc < 8:
                return hT[:, kc, ch * 512:(ch + 1) * 512], ("hT", kc, ch)
            return wb0v[:, kc - 8, ch * 512:(ch + 1) * 512], ("win", 0)

        if L == 256:
            load_wout()
        hmix = (lambda hh_, c2_: 4 + hh_ * 2 + c2_) if gi == 0 else (lambda hh_, c2_: 6)
        if L != 256 and HH == 1:
            DMA(dft.rearrange("p (t r) -> p t r", t=8), fwd_d[L].rearrange("(t p) r -> p t r", p=128), (), ["dft"], q="pool")
        for hh in range(HH):
            pf("hh%d" % hh)
            for c2 in range(NC2):
                chn = 0 * 4 + hh * 2 + c2
                def ev_x0(hf, b, c2=c2, chn=chn):
                    sl = slice(hf * 512, (hf + 1) * 512)
                    TS("dve", x0c[:, c2, sl], PB(b)[:, :], pp[:, PC_HSB + chn:PC_HSB + chn + 1], None, ALU.add, None, [("ps", b), "pp"], [("x0c", c2)])
                proj_conv(LOC["x0"](hh, c2), 3, (lambda j, chn=chn: pp[:, PC_HSW + j * 12 + chn:PC_HSW + j * 12 + chn + 1]), ev_x0)
            ykeys = [("Y", s_) for s_ in range(nseq)]
            for c2 in range(NC2):
                chn = 1 * 4 + hh * 2 + c2
                def ev_x1(hf, b, chn=chn):
                    sl = slice(hf * 512, (hf + 1) * 512)
                    TS("dve", cvt[:, sl], PB(b)[:, :], pp[:, PC_HSB + chn:PC_HSB + chn + 1], None, ALU.add, None, [("ps", b), "pp"], ykeys)
                proj_conv(LOC["x1"](hh, c2), 3, (lambda j, chn=chn: pp[:, PC_HSW + j * 12 + chn:PC_HSW + j * 12 + chn + 1]), ev_x1)
                chn = 8 + hh * 2 + c2
                def ev_v(hf, b, c2=c2, chn=chn):
                    sl = slice(hf * 512, (hf + 1) * 512)
                    STT("dve", zT[:, c2, sl], PB(b)[:, :], pp[:, PC_HSB + chn:PC_HSB + chn + 1], cvt[:, sl], ALU.add, ALU.mult,
                        [("ps", b), "pp"] + ykeys, [("zT", c2)])
                proj_conv(LOC["hv"](hh, c2), 3, (lambda j, chn=chn: pp[:, PC_HSW + j * 12 + chn:PC_HSW + j * 12 + chn + 1]), ev_v)
            pf("after_hx%d" % hh)
            for c2 in range(NC2):
                for hf in range(2):
                    b = inproj(LOC["hg"](hh, c2), hf)
                    ACT(sgh[:, c2, hf * 512:(hf + 1) * 512], PB(b)[:, :], AF.Silu, [("ps", b)], [("sgh", c2, hf)])
            if gi == 1 and hh == HH - 1:
                load_wout()
            for j in range(8):
                b = bank("tr")
                pbv = PB(b)[:, :].bitcast(BF16)
                for c2 in range(NC2):
                    TR(pbv[:, c2 * 128:(c2 + 1) * 128], zT[:, c2, j * 128:(j + 1) * 128], [("zT", c2)], [("ps", b)])
                CP("act", ztok[:, j, 0:CW], pbv[:, 0:CW], [("ps", b)], [("ztok", j)])
            DMA(w3s[:, 0:2 * CW], (w3_d if gi == 0 else w3ss_d)[l][:, hh * 512:hh * 512 + 2 * CW], (), ["w3s"])
            DMA(decbc[:, 0:2 * CW], (dec_d if gi == 0 else decs_d)[l][hh * 512:hh * 512 + 2 * CW].partition_broadcast(128), (), ["decbc"])
            bN = 7
            W2 = 2 * CW
            sc3b = sc3[:, 0:256].bitcast(BF16)

            def taps(lt_):
                b_ = bank("mm")
                MM(PB(b_)[:, 0:W2], hdn1[0:64, lt_ * 128:(lt_ + 1) * 128], w3s[0:64, 0:W2], True, True,
                   ["hdn1", "w3s"], [("ps", b_)])
                return b_

            bnext = taps(0)
            for lt in range(nf):
                b = bnext
                if lt + 1 < nf:
                    bnext = taps(lt + 1)
                ACT(sc1[:, 0:W2], decbc[:, 0:W2], AF.Exp, ["decbc", "ntcol"], ["sc1"], scale=ntcol[:, lt:lt + 1])
                TT("dve", sc2[:, 0:W2], PB(b)[:, 0:W2], sc1[:, 0:W2], ALU.mult, [("ps", b), "sc1"], ["sc2"])
                if lt == 0:
                    MEMSET("dve", sc2[0:1, CW:W2], 0.0, ["sc2"])
                STT("pool", sc3b[:, 0:W2], sc2[:, 0:W2], -1.0, sc2[:, 0:W2], ALU.mult, ALU.max, ["sc2"], ["sc3"])
                MM(PB(bN)[:, 0:W2], onesb[:], sc3b[:, 0:W2], lt == 0, lt == nf - 1, ["onesb", "sc3"], [("ps", bN)])
                TT("pool", filt[:, lt, 0:CW], sc2[:, 0:CW], sc2[:, CW:W2], ALU.add, ["sc2"], [("filt", lt)])
                TT("pool", filt[:, lt, 256:256 + CW], sc2[:, 0:CW], sc2[:, CW:W2], ALU.subtract, ["sc2"], [("filt", lt)])
            CP("act", sc1[:, 0:2 * CW], PB(bN)[:, 0:2 * CW], [("ps", bN)], ["sc1"])
            TT("dve", rnb[:, 0:CW], sc1[:, 0:CW], sc1[:, CW:2 * CW], ALU.add, ["sc1"], ["rnb"])
            S.op("dve", (lambda cw: lambda e: e.reciprocal(out=rnb[:, 0:cw], in_=rnb[:, 0:cw]))(CW), ["rnb"], ["rnb"])
            if L != 256 and HH != 1:
                DMA(dft.rearrange("p (t r) -> p t r", t=8), fwd_d[L].rearrange("(t p) r -> p t r", p=128), (), ["dft"])
            Yv = Ypk[:, 0:nseq * 2 * nf * CW].rearrange("p (s r c) -> p s r c", s=nseq, r=2 * nf)
            for i in range(nf):
                b = bank("mm")
                for jt in range(nf):
                    MM(PB(b)[:, 0:CW], fwdv(jt, i * 128), filt[:, jt, 0:CW], jt == 0, jt == nf - 1, [dkey, ("filt", jt)], [("ps", b)])
                for jt in range(nf):
                    MM(PB(b)[:, 256:256 + CW], fwdv(jt, (nf + i) * 128), filt[:, jt, 256:256 + CW], jt == 0, jt == nf - 1, [dkey, ("filt", jt)], [("ps", b)])
                TT("dve", Hs[:, 0, 0:CW], PB(b)[:, 0:CW], rnb[:, 0:CW], ALU.mult, [("ps", b), "rnb"], ["Hs"])
                TT("dve", Hs[:, 1, 0:CW], PB(b)[:, 256:256 + CW], rnb[:, 0:CW], ALU.mult, [("ps", b), "rnb"], ["Hs"])
                if i == 0:
                    b2 = bank("mm")
                    for jt in range(nf):
                        MM(PB(b2)[:, 0:CW], fwdv(jt, nf * 128), filt[:, jt, 0:CW], jt == 0, jt == nf - 1, [dkey, ("filt", jt)], [("ps", b2)])
                    TT("dve", Hs[0:1, 1, 0:CW], PB(b2)[0:1, 0:CW], rnb[0:1, 0:CW], ALU.mult, [("ps", b2), "rnb", "Hs"], ["Hs"])
                for s in range(nseq):
                    bz = bank("st")
                    for jt in range(nf):
                        MM(PB(bz)[:, 0:CW], fwdv(jt, i * 128), ztok[:, s * ntl + jt, 0:CW], jt == 0, jt == nf - 1, [dkey, ("ztok", s * ntl + jt)], [("ps", bz)])
                    for jt in range(nf):
                        MM(PB(bz)[:, 256:256 + CW], fwdv(jt, (nf + i) * 128), ztok[:, s * ntl + jt, 0:CW], jt == 0, jt == nf - 1, [dkey, ("ztok", s * ntl + jt)], [("ps", bz)])
                    zre, zim = PB(bz)[:, 0:CW], PB(bz)[:, 256:256 + CW]
                    zk = [("ps", bz)]
                    TT("dve", hyt1[:, 0:CW], zre, Hs[:, 0, 0:CW], ALU.mult, zk + ["Hs"], ["hyt1"])
                    TT("dve", hyt2[:, 0:CW], zim, Hs[:, 1, 0:CW], ALU.mult, zk + ["Hs"], ["hyt2"])
                    TT("pool", Yv[:, s, i, :], hyt1[:, 0:CW], hyt2[:, 0:CW], ALU.subtract, ["hyt1", "hyt2"], [("Y", s)])
                    TT("dve", hyt3[:, 0:CW], zre, Hs[:, 1, 0:CW], ALU.mult, zk + ["Hs"], ["hyt3"])
                    TT("dve", sc3[:, 0:CW], zim, Hs[:, 0, 0:CW], ALU.mult, zk + ["Hs"], ["sc3"])
                    TT("pool", Yv[:, s, nf + i, :], hyt3[:, 0:CW], sc3[:, 0:CW], ALU.add, ["hyt3", "sc3"], [("Y", s)])
                    if i == 0:
                        TT("dve", Yv[0:1, s, 0, :], PB(bz)[0:1, 0:CW], Hs[0:1, 0, 0:CW], ALU.mult, zk + ["Hs", ("Y", s)], [("Y", s)])
                        TT("dve", Yv[0:1, s, nf, :], PB(bz)[0:1, 256:256 + CW], Hs[0:1, 1, 0:CW], ALU.mult, zk + ["Hs", ("Y", s)], [("Y", s)])
            if L != 256:
                DMA(dft.rearrange("p (t r) -> p t r", t=16), inv_d[L].rearrange("(t p) r -> p t r", p=128), (), ["dft"])
            for s in range(nseq):
                for c2 in range(NC2):
                    for tb in range(nqb):
                        tt0 = s * L + tb * NQ
                        hf = tt0 // 512
                        b = bank("mm")
                        for rc in range(2 * nf):
                            MM(PB(b)[:, 0:NQ], Yv[:, s, rc, c2 * 128:(c2 + 1) * 128], invv(rc, tb * NQ, NQ), rc == 0, rc == 2 * nf - 1,
                               [("Y", s), ikey], [("ps", b)])
                        hbcol = PC_HYB + hh * 2 + c2
                        STT("dve", sc1[:, 0:NQ], zT[:, c2, tt0:tt0 + NQ], pp[:, hbcol:hbcol + 1], PB(b)[:, 0:NQ], ALU.mult, ALU.add,
                            [("zT", c2), "pp", ("ps", b)], ["sc1"])
                        TT("pool", sc2[:, 0:NQ], sc1[:, 0:NQ], x0c[:, c2, tt0:tt0 + NQ], ALU.mult, ["sc1", ("x0c", c2)], ["sc2"])
                        TT("pool", mixT[:, hmix(hh, c2), tt0:tt0 + NQ], sc2[:, 0:NQ], sgh[:, c2, tt0:tt0 + NQ], ALU.mult,
                           ["sc2", ("sgh", c2, hf)], [("mix", hmix(hh, c2), hf)])

        CK(5 + 10 * (gi * 2 + l))
        if gi == 1:
            gather(1, 6)
        kc_sets = [list(range(12))] if gi == 0 else [[0, 1, 2, 3, 8, 9, 10, 11], [4, 5, 6, 7]]
        for kcs in kc_sets:
            for j in range(8):
                for ch in range(2):
                    b = bank("mm")
                    for ki, kc in enumerate(kcs):
                        rhs_, rk_ = wo_rhs(kc, ch)
                        MM(PB(b)[:, :], mixT[:, kc, j * 128:(j + 1) * 128], rhs_, ki == 0, ki == len(kcs) - 1,
                           [("mix", kc, j // 4), rk_], [("ps", b)])
                    xs_ = x_t[:, j, ch * 512:(ch + 1) * 512]
                    TT("dve", sc1[:, :], PB(b)[:, :], gate_bc[:, ch * 512:(ch + 1) * 512], ALU.mult, [("ps", b), ("gate", ch)], ["sc1"])
                    TT("pool", xs_, xs_, sc1[:, :], ALU.add, [("x", j), "sc1"], [("x", j)])

    def conv3(src, skey, dst, dkey_, chn, nseq, L):
        dkl = list(dkey_) if isinstance(dkey_, list) else [dkey_]
        w = lambda j: pp[:, PC_HSW + j * 12 + chn:PC_HSW + j * 12 + chn + 1]
        src3 = src.rearrange("p (s n) -> p s n", s=nseq)
        dst3 = dst.rearrange("p (s n) -> p s n", s=nseq)
        TS("pool", dst, src, w(1), pp[:, PC_HSB + chn:PC_HSB + chn + 1], ALU.mult, ALU.add, [skey, "pp"], dkl)
        STT("pool", dst3[:, :, 1:L], src3[:, :, 0:L - 1], w(0), dst3[:, :, 1:L], ALU.mult, ALU.add, [skey, "pp"] + dkl, dkl)
        STT("pool", dst3[:, :, 0:L - 1], src3[:, :, 1:L], w(2), dst3[:, :, 0:L - 1], ALU.mult, ALU.add, [skey, "pp"] + dkl, dkl)

    ropec = sb("ropec", [128, 1024], BF16)
    ropes = sb("ropes", [128, 1024], BF16)
    fwd256 = ropec[:, :].rearrange("p (t r) -> p t r", t=2)
    inv256 = ropes[:, :].rearrange("p (t r) -> p t r", t=4)
    DMA(fwd256, fwd_d[256].rearrange("(t p) r -> p t r", p=128), (), ["ropec"])
    DMA(inv256, inv_d[256].rearrange("(t p) r -> p t r", p=128), (), ["ropes"])
    wo_full = ar_a[:, 0:6144].bitcast(BF16).rearrange("p (k c) -> p k c", k=12)
    S.alias["wof"] = ("ar_a", "op")

    groups = [(0, 4, 256), (1, 1, 1024)]
    try:
      startup()
      CK(1)
      for gi, nseq, L in groups:
          if gi == 1:
              DMA(ropec[:], cos_d, (), ["ropec"])
              DMA(ropes[:], sin_d, (), ["ropes"])
          if gi == 0:
              for j in range(8):
                  DMA(x_t[:, j, :], xg_d[gi][j * 128:(j + 1) * 128, :], (), [("x", j)])
          for l in range(2):
              layer_pass(gi, l, nseq, L)
          DMA(gate_bc[:], fg_d.partition_broadcast(128), (), [("gate", 0), ("gate", 1)])
          junk = ar_c[:, 0:512].bitcast(BF16)
          for j in range(8):
              ACT(junk, x_t[:, j, :], AF.Square, [("x", j)], ["modtmp", ("ssqf", j)], accum=ssq[:, j:j + 1])
          ACT(rstd[:, 0:8], ssq[:, 0:8], AF.Ln, [("ssqf", j_) for j_ in range(8)] + ["small"], [("rstdf", 0)], scale=1.0 / D, bias=epsc)
          ACT(rstd[:, 0:8], rstd[:, 0:8], AF.Exp, [("rstdf", 0)], [("rstdf", 0)], scale=-0.5)
          hT32 = hT[:, :, :].rearrange("p k n -> p (k n)").bitcast(F32).rearrange("p (t n) -> p t n", t=4)
          for j in range(8):
              st_ = j % 6
              if st_ < 2:
                  yst, yk = ust[:, st_, :], [("ust", st_)]
              else:
                  t_ = st_ - 2
                  yst = hT32[:, t_, :]
                  yk = [("hT", 2 * t_ + a_, b_) for a_ in range(2) for b_ in range(2)]
              STT("dve", yst, x_t[:, j, :], rstd[:, j:j + 1], gate_bc[:], ALU.mult, ALU.mult, [("x", j), ("rstdf", 0), ("gate", 0), ("gate", 1)], yk)
              if gi == 0:
                  DMA(x_t[:, j, :], xg_d[1][j * 128:(j + 1) * 128, :], (), [("x", j)])
              DMA(yg_d[gi][j * 128:(j + 1) * 128, :], yst, yk, ())
    except _StopBuild:
        pass
    with nc.allow_non_contiguous_dma(reason="tiny strided state/param transfers"):
        info = S.finalize()
    info["sbuf_left"] = nc.sbuf_bytes_remaining
    return nc, info


def _consts():
    c = {}
    c["ident"] = np.eye(128, dtype=np.float32).astype(ml_dtypes.bfloat16)
    p = np.zeros((128, 128), np.float32)
    for m in range(128):
        p[m ^ 1, m] = 1.0
    c["psw"] = p.astype(ml_dtypes.bfloat16)
    bo = np.zeros((128, 128), np.float32)
    bo[0:64, 0:64] = 1.0
    bo[64:128, 64:128] = 1.0
    c["bones"] = bo.astype(ml_dtypes.bfloat16)
    for L in (256, 1024):
        t = np.linspace(0.0, 1.0, L, dtype=np.float32)
        w = (2.0 * np.pi * np.arange(L, dtype=np.float32) / L).astype(np.float32)
        f = np.linspace(1e-4, 15, 16, dtype=np.float32)
        fw = (f[None, :] * w[:, None]).astype(np.float32)
        feats = np.concatenate([t[:, None], np.cos(fw), -np.sin(fw)], axis=-1).astype(np.float32)
        c["ft%d" % L] = np.ascontiguousarray(feats.T)
        c["tc%d" % L] = np.ascontiguousarray(t.reshape(L // 128, 128).T)
        th = np.pi / L
        j = np.arange(L, dtype=np.float64)[:, None]
        fwd = np.zeros((L, 2 * L), np.float64)
        fr = np.arange(L, dtype=np.float64)[None, :]
        fwd[:, 0:L] = np.cos(th * j * fr)
        fwd[:, L] = np.cos(np.pi * j[:, 0])
        fi = np.arange(1, L, dtype=np.float64)[None, :]
        fwd[:, L + 1:2 * L] = -np.sin(th * j * fi)
        c["fwd%d" % L] = fwd.astype(np.float32).astype(ml_dtypes.bfloat16)
        tt = np.arange(L, dtype=np.float64)[None, :]
        inv = np.zeros((2 * L, L), np.float64)
        fcol = np.arange(L, dtype=np.float64)[:, None]
        inv[0:L, :] = np.cos(th * fcol * tt) / L
        inv[0, :] = 1.0 / (2 * L)
        inv[L, :] = np.cos(np.pi * tt[0]) / (2 * L)
        fic = np.arange(1, L, dtype=np.float64)[:, None]
        inv[L + 1:2 * L, :] = -np.sin(th * fic * tt) / L
        c["inv%d" % L] = inv.astype(np.float32).astype(ml_dtypes.bfloat16)
    tok = np.arange(1024)
    row = (tok // 64).astype(np.float32)
    col = (tok % 64).astype(np.float32)
    nfq = 16
    invf = (10000.0 ** (-np.arange(nfq, dtype=np.float32) / nfq)).astype(np.float32)
    ang = np.concatenate([row[:, None] * invf, col[:, None] * invf], axis=-1).astype(np.float32)
    cosv = np.cos(ang).astype(np.float32)
    sinv = np.sin(ang).astype(np.float32)
    rc = np.zeros((128, 1024), np.float32)
    rs = np.zeros((128, 1024), np.float32)
    for p_ in range(128):
        d = p_ % 64
        i = d // 2
        rc[p_] = cosv[:, i]
        rs[p_] = (-sinv[:, i]) if d % 2 == 0 else sinv[:, i]
    c["ropec"] = rc.astype(ml_dtypes.bfloat16)
    c["ropes"] = rs.astype(ml_dtypes.bfloat16)
    return c


_CACHE = {}


def kernel(x_prompt, x_sample, cache_k, cache_v, state_lru, c, c_ctx, norm_g, w_ada, b_ada, w_in,
           q_norm_g, k_norm_g, hy_short_w, hy_short_b, hy_filt_w1, hy_filt_b1, hy_filt_w2, hy_filt_b2,
           hy_filt_w3, hy_filt_freq, hy_filt_decay, hy_bias, lru_conv_w, lru_conv_b, lru_wa, lru_ba,
           lru_wx, lru_bx, lru_lambda, w_out, final_g):
    f = lambda a: np.ascontiguousarray(np.asarray(a, dtype=np.float32))
    x_prompt, x_sample, cache_k, cache_v, state_lru, c, c_ctx = map(f, (x_prompt, x_sample, cache_k, cache_v, state_lru, c, c_ctx))
    w_in = f(w_in)
    Q0, K0, V0, GA0, HX0, HX1, HV, GH, LX, GL = 0, 512, 640, 768, 1280, 1792, 2304, 2816, 3328, 3840
    cols = []
    cols += list(range(Q0, Q0 + 512))
    cols += list(range(K0, K0 + 64)) * 2 + list(range(K0 + 64, K0 + 128)) * 2
    cols += list(range(V0, V0 + 64)) * 2 + list(range(V0 + 64, V0 + 128)) * 2
    cols += list(range(GA0, GA0 + 512))
    cols += list(range(LX, LX + 512))
    cols += list(range(GL, GL + 512))
    for hh in range(2):
        cols += list(range(HX0 + hh * 256, HX0 + hh * 256 + 256)) + list(range(HX1 + hh * 256, HX1 + hh * 256 + 256))
        cols += list(range(HV + hh * 256, HV + hh * 256 + 256)) + list(range(GH + hh * 256, GH + hh * 256 + 256))
    cols = np.array(cols)
    assert cols.size == NG * 512
    w_in_r = np.ascontiguousarray(w_in[:, :, cols])
    ppack = np.zeros((2, 128, PC_N), np.float32)
    lru_bd = np.zeros((2, 128, 16, 128), np.float32)
    w3r = np.zeros((2, 64, 1024), np.float32)
    decr = np.zeros((2, 1024), np.float32)
    g = lambda a: np.asarray(a, dtype=np.float32)
    for l in range(2):
        ppack[l, :, PC_NG:PC_NG + 8] = g(norm_g)[l].reshape(8, 128).T
        ppack[l, :, PC_HSW:PC_HSW + 36] = g(hy_short_w)[l].reshape(3, 12, 128).transpose(2, 0, 1).reshape(128, 36)
        ppack[l, :, PC_HSB:PC_HSB + 12] = g(hy_short_b)[l].reshape(12, 128).T
        ppack[l, :, PC_HYB:PC_HYB + 4] = g(hy_bias)[l].reshape(4, 128).T
        ppack[l, :, PC_LCW:PC_LCW + 16] = g(lru_conv_w)[l].reshape(4, 4, 128).transpose(2, 0, 1).reshape(128, 16)
        ppack[l, :, PC_LCB:PC_LCB + 4] = g(lru_conv_b)[l].reshape(4, 128).T
        ppack[l, :, PC_LBA:PC_LBA + 8] = g(lru_ba)[l].reshape(2, 4, 128).transpose(2, 0, 1).reshape(128, 8)
        ppack[l, :, PC_LBX:PC_LBX + 8] = g(lru_bx)[l].reshape(2, 4, 128).transpose(2, 0, 1).reshape(128, 8)
        ppack[l, :, PC_LAM:PC_LAM + 8] = g(lru_lambda)[l].reshape(2, 4, 128).transpose(2, 0, 1).reshape(128, 8)
        ppack[l, :, PC_QG] = np.tile(g(q_norm_g)[l], 2)
        ppack[l, :, PC_KG] = np.tile(g(k_norm_g)[l], 2)
        ppack[l, 0:64, PC_B1] = g(hy_filt_b1)[l]
        ppack[l, 0:64, PC_B2] = g(hy_filt_b2)[l]
        ppack[l, 0:64, PC_F0] = g(hy_filt_freq)[l, 0]
        ppack[l, 0:64, PC_F1] = g(hy_filt_freq)[l, 1]
        for d in range(2):
            for gt, wsrc in enumerate((g(lru_wa), g(lru_wx))):
                for cc in range(4):
                    n = (d * 2 + gt) * 4 + cc
                    lru_bd[l, 0:64, n, 0:64] = wsrc[l, d, 2 * cc]
                    lru_bd[l, 64:128, n, 64:128] = wsrc[l, d, 2 * cc + 1]
        w3 = g(hy_filt_w3)[l]
        dc = g(hy_filt_decay)[l]
        for hh in range(2):
            w3r[l, :, hh * 512:hh * 512 + 256] = w3[:, hh * 256:hh * 256 + 256]
            w3r[l, :, hh * 512 + 256:hh * 512 + 512] = w3[:, 512 + hh * 256:512 + hh * 256 + 256]
            decr[l, hh * 512:hh * 512 + 256] = dc[hh * 256:hh * 256 + 256]
            decr[l, hh * 512 + 256:hh * 512 + 512] = dc[512 + hh * 256:512 + hh * 256 + 256]
    perm = []
    for r_ in range(4):
        perm += list(range(r_ * 128, r_ * 128 + 128)) + list(range(512 + r_ * 128, 512 + r_ * 128 + 128)) + list(range(1024 + r_ * 128, 1024 + r_ * 128 + 128))
    w_out_s = np.ascontiguousarray(f(w_out)[:, np.array(perm), :])
    share = []
    for r_ in range(4):
        gk = r_ // 2
        sc_ = []
        sc_ += list(range(Q0 + r_ * 128, Q0 + r_ * 128 + 128))
        sc_ += list(range(K0 + gk * 64, K0 + gk * 64 + 64)) * 2
        sc_ += list(range(V0 + gk * 64, V0 + gk * 64 + 64)) * 2
        sc_ += list(range(GA0 + r_ * 128, GA0 + r_ * 128 + 128))
        sc_ += list(range(LX + r_ * 128, LX + r_ * 128 + 128))
        sc_ += list(range(GL + r_ * 128, GL + r_ * 128 + 128))
        sc_ += list(range(HX0 + r_ * 128, HX0 + r_ * 128 + 128))
        sc_ += list(range(HX1 + r_ * 128, HX1 + r_ * 128 + 128))
        sc_ += list(range(HV + r_ * 128, HV + r_ * 128 + 128))
        sc_ += list(range(GH + r_ * 128, GH + r_ * 128 + 128))
        wsh = np.zeros((2, D, 1536), np.float32)
        wsh[:, :, 0:1280] = w_in[:, :, np.array(sc_)]
        pps = ppack.copy()
        bds = np.zeros((2, 128, 16, 128), np.float32)
        w3ss = np.zeros((2, 64, 256), np.float32)
        decs = np.zeros((2, 256), np.float32)
        for l in range(2):
            for j_ in range(3):
                pps[l, :, PC_HSW + j_ * 12 + 0] = g(hy_short_w)[l, j_, r_ * 128:(r_ + 1) * 128]
                pps[l, :, PC_HSW + j_ * 12 + 4] = g(hy_short_w)[l, j_, 512 + r_ * 128:512 + (r_ + 1) * 128]
                pps[l, :, PC_HSW + j_ * 12 + 8] = g(hy_short_w)[l, j_, 1024 + r_ * 128:1024 + (r_ + 1) * 128]
            pps[l, :, PC_HSB + 0] = g(hy_short_b)[l, r_ * 128:(r_ + 1) * 128]
            pps[l, :, PC_HSB + 4] = g(hy_short_b)[l, 512 + r_ * 128:512 + (r_ + 1) * 128]
            pps[l, :, PC_HSB + 8] = g(hy_short_b)[l, 1024 + r_ * 128:1024 + (r_ + 1) * 128]
            pps[l, :, PC_HYB + 0] = g(hy_bias)[l, r_ * 128:(r_ + 1) * 128]
            for j_ in range(4):
                pps[l, :, PC_LCW + j_ * 4 + 0] = g(lru_conv_w)[l, j_, r_ * 128:(r_ + 1) * 128]
            pps[l, :, PC_LCB + 0] = g(lru_conv_b)[l, r_ * 128:(r_ + 1) * 128]
            for d in range(2):
                pps[l, :, PC_LBA + d * 4 + 0] = g(lru_ba)[l, d, r_ * 128:(r_ + 1) * 128]
                pps[l, :, PC_LBX + d * 4 + 0] = g(lru_bx)[l, d, r_ * 128:(r_ + 1) * 128]
                pps[l, :, PC_LAM + d * 4 + 0] = g(lru_lambda)[l, d, r_ * 128:(r_ + 1) * 128]
                for gt, wsrc in enumerate((g(lru_wa), g(lru_wx))):
                    n = (d * 2 + gt) * 4 + 0
                    bds[l, 0:64, n, 0:64] = wsrc[l, d, 2 * r_]
                    bds[l, 64:128, n, 64:128] = wsrc[l, d, 2 * r_ + 1]
            w3 = g(hy_filt_w3)[l]
            dc = g(hy_filt_decay)[l]
            w3ss[l, :, 0:128] = w3[:, r_ * 128:(r_ + 1) * 128]
            w3ss[l, :, 128:256] = w3[:, 512 + r_ * 128:512 + (r_ + 1) * 128]
            decs[l, 0:128] = dc[r_ * 128:(r_ + 1) * 128]
            decs[l, 128:256] = dc[512 + r_ * 128:512 + (r_ + 1) * 128]
        share.append(dict(w_in_s=wsh, pps=pps, bds=bds.reshape(2, 128, 2048), w3ss=w3ss, decs=decs))
    consts = _consts()
    shared = dict(w_out_s=w_out_s, w_ada=f(w_ada), b_ada=f(b_ada), w_in_r=w_in_r, w_out=f(w_out), ppack=ppack,
                  lru_bd=lru_bd.reshape(2, 128, 2048), w1=f(hy_filt_w1), w2=f(hy_filt_w2), w3r=w3r, decr=decr,
                  final_g=f(final_g), **consts)
    in_maps = []
    for ci in range(NCORES):
        b = ci % 2
        m = dict(shared)
        m["xp"] = x_prompt[4 * ci:4 * ci + 4].reshape(1024, D)
        m["xs"] = x_sample[b]
        r_ = ci // 2
        gk = r_ // 2
        m.update(share[r_])
        m["ckT"] = np.ascontiguousarray(cache_k[b][:, :, gk, :].transpose(0, 2, 1))
        m["cv"] = np.ascontiguousarray(cache_v[b][:, :, gk, :])
        stt = np.zeros((128, 16), np.float32)
        for l_ in range(2):
            for d_ in range(2):
                stt[:, (l_ * 2 + d_) * 4 + 0] = state_lru[b, l_, d_, r_ * 128:(r_ + 1) * 128]
        m["st"] = stt
        cp = np.stack([c_ctx.reshape(8, 128), c[b].reshape(8, 128)], axis=-1)
        m["cpack"] = np.ascontiguousarray(cp.transpose(1, 0, 2).reshape(128, 16))
        in_maps.append(m)
    if "nc" not in _CACHE:
        _CACHE["nc"], _CACHE["info"] = build_program()
    nc = _CACHE["nc"]
    res = run_bass_kernel_spmd(nc, in_maps, core_ids=list(range(NCORES)))
    R = res.results
    y_prompt = np.concatenate([R[i]["yp"].reshape(4, 256, D) for i in range(NCORES)], axis=0)
    y_sample = np.stack([R[0]["ys"], R[1]["ys"]], axis=0)
    nk = np.concatenate([R[i]["nk"].reshape(2, 4, 256, 2, 64).transpose(1, 0, 2, 3, 4) for i in range(NCORES)], axis=0)
    nv = np.concatenate([R[i]["nv"].reshape(2, 4, 256, 2, 64).transpose(1, 0, 2, 3, 4) for i in range(NCORES)], axis=0)
    nl = np.concatenate([R[i]["nl"].reshape(2, 4, 2, 512).transpose(1, 0, 2, 3) for i in range(NCORES)], axis=0)
    return (y_prompt.astype(np.float32), y_sample.astype(np.float32), np.ascontiguousarray(nk, dtype=np.float32),
            np.ascontiguousarray(nv, dtype=np.float32), np.ascontiguousarray(nl, dtype=np.float32))
```
